# Optimizing a Trainium2 kernel written in Bass

```python
import math
import jax
import jax.numpy as jnp
from jax import lax
import numpy as np

D_MODEL = 1024
BATCH = 2
SEQ = 8192
DEPTH = 4

GRID_W = 64
CTX_LEN = 256

SSM_WIDTH = 512
SSM_GROUP = 16
SSM_GROUPS = SSM_WIDTH // SSM_GROUP
SSM_STATE = 64
SSM_DT_MIN = 1e-3
SSM_DT_MAX = 1e-1

NA_HEADS = 8
NA_HEAD_DIM = 64
NA_WIDTH = NA_HEADS * NA_HEAD_DIM
NA_WIN_ROWS = 8
NA_WIN_COLS = 16

MLA_HEADS = 8
MLA_Q_RANK = 256
MLA_KV_RANK = 128
MLA_NOPE_DIM = 64
MLA_ROPE_DIM = 32
MLA_V_DIM = 64
MLA_QK_DIM = MLA_NOPE_DIM + MLA_ROPE_DIM
MLA_WIDTH = MLA_HEADS * MLA_V_DIM

Q_BLOCK = 128
ROPE_THETA = 10000.0
LN_EPS = 1e-5
RMS_EPS = 1e-6
NEG_INF = -1e30
DEEPNORM_ALPHA = (2 * DEPTH) ** 0.25
DEEPNORM_BETA = (8 * DEPTH) ** -0.25
NA_SCALE = NA_HEAD_DIM ** -0.5
MLA_SCALE = MLA_QK_DIM ** -0.5
F32 = jnp.float32

IN_SPLITS = (SSM_WIDTH, SSM_WIDTH,
             NA_WIDTH, NA_WIDTH, NA_WIDTH, NA_WIDTH,
             MLA_Q_RANK, MLA_KV_RANK, MLA_ROPE_DIM, MLA_WIDTH,
             D_MODEL, D_MODEL, D_MODEL)
IN_WIDTH = sum(IN_SPLITS)

kernel_name = 'hybrid_s5_natten_mla_prefix_block'


def layer_norm(x, g=None, b=None):
    xf = x.astype(F32)
    xc = xf - jnp.mean(xf, axis=-1, keepdims=True)
    y = xc * lax.rsqrt(jnp.mean(xc * xc, axis=-1, keepdims=True) + LN_EPS)
    if g is not None:
        y = y * g.astype(F32) + b.astype(F32)
    return y.astype(x.dtype)


def rms_norm(x, g):
    xf = x.astype(F32)
    y = xf * lax.rsqrt(jnp.mean(xf * xf, axis=-1, keepdims=True) + RMS_EPS) * g.astype(F32)
    return y.astype(x.dtype)


def split_cols(p):
    idx = [int(i) for i in np.cumsum(IN_SPLITS)[:-1]]
    return jnp.split(p, idx, axis=-1)


def axial_rope_tables(n_tokens):
    t = jnp.arange(n_tokens, dtype=jnp.int32)
    row = (t // GRID_W).astype(F32)
    col = (t % GRID_W).astype(F32)
    half = MLA_ROPE_DIM // 2
    inv = 1.0 / (ROPE_THETA ** (jnp.arange(0, half, 2, dtype=F32) / half))
    ang_r = row[:, None] * inv[None, :]
    ang_c = col[:, None] * inv[None, :]
    return (jnp.cos(ang_r), jnp.sin(ang_r), jnp.cos(ang_c), jnp.sin(ang_c))


def _rotate(x, cos, sin):
    n = x.shape[-1] // 2
    x1 = x[..., :n].astype(F32)
    x2 = x[..., n:].astype(F32)
    cos = cos[None, :, None, :]
    sin = sin[None, :, None, :]
    return jnp.concatenate([x1 * cos - x2 * sin, x1 * sin + x2 * cos], axis=-1).astype(x.dtype)


def apply_axial_rope(x, rope):
    cos_r, sin_r, cos_c, sin_c = rope
    half = x.shape[-1] // 2
    return jnp.concatenate([_rotate(x[..., :half], cos_r, sin_r),
                            _rotate(x[..., half:], cos_c, sin_c)], axis=-1)


def block_attention(q, k, v, scale):
    B, S, H, dq = q.shape
    nb = S // Q_BLOCK
    qb = jnp.moveaxis(q.reshape(B, nb, Q_BLOCK, H, dq), 1, 0)

    def attend(q_blk):
        s = jnp.einsum('bqhd,bkhd->bhqk', q_blk, k).astype(F32) * scale
        p = jax.nn.softmax(s, axis=-1).astype(v.dtype)
        return jnp.einsum('bhqk,bkhd->bqhd', p, v)

    o = lax.map(attend, qb)
    return jnp.moveaxis(o, 0, 1).reshape(B, S, H * v.shape[-1])


def _complex_combine(e1, e2):
    a1r, a1i, b1r, b1i = e1
    a2r, a2i, b2r, b2i = e2
    return (a2r * a1r - a2i * a1i, a2r * a1i + a2i * a1r,
            a2r * b1r - a2i * b1i + b2r, a2r * b1i + a2i * b1r + b2i)


def s5_discretise(lam_re, lam_im, log_dt, b_re, b_im):
    dt = jnp.exp(log_dt.astype(F32))[:, None]
    lr = lam_re.astype(F32)
    li = lam_im.astype(F32)
    mag = jnp.exp(lr * dt)
    ab_re = mag * jnp.cos(li * dt)
    ab_im = mag * jnp.sin(li * dt)
    nr = ab_re - 1.0
    den = lr * lr + li * li
    fr = (nr * lr + ab_im * li) / den
    fi = (ab_im * lr - nr * li) / den
    br = b_re.astype(F32)
    bi = b_im.astype(F32)
    bb_re = fr[..., None] * br - fi[..., None] * bi
    bb_im = fr[..., None] * bi + fi[..., None] * br
    return ab_re, ab_im, bb_re, bb_im


def s5_states(u, disc, s0):
    ab_re, ab_im, bb_re, bb_im = disc
    bu_re = jnp.einsum('blgi,gpi->blgp', u, bb_re)
    bu_im = jnp.einsum('blgi,gpi->blgp', u, bb_im)
    if s0 is not None:
        s0_re, s0_im = s0
        bu_re = bu_re.at[:, 0].add(ab_re * s0_re - ab_im * s0_im)
        bu_im = bu_im.at[:, 0].add(ab_re * s0_im + ab_im * s0_re)
    a_re = jnp.broadcast_to(ab_re, bu_re.shape)
    a_im = jnp.broadcast_to(ab_im, bu_im.shape)
    _, _, s_re, s_im = lax.associative_scan(_complex_combine, (a_re, a_im, bu_re, bu_im), axis=1)
    return s_re, s_im


def s5_readout(states, c_re, c_im):
    s_re, s_im = states
    return jnp.einsum('gip,blgp->blgi', c_re, s_re) - jnp.einsum('gip,blgp->blgi', c_im, s_im)


def s5_output(y, u, d_skip, w_glu, b_glu):
    B, L, W = u.shape
    y = (y.reshape(B, L, W) + d_skip.astype(F32) * u.astype(F32)).astype(u.dtype)
    y = jax.nn.gelu(y)
    return y * jax.nn.sigmoid(y @ w_glu + b_glu)


def s5_mixer(u_lat, u_ctx, lam_re, lam_im, log_dt, b_re, b_im, c_re, c_im,
             d_skip, w_glu, b_glu, need_ctx):
    B, S, _ = u_lat.shape
    ul = u_lat.astype(F32).reshape(B, S, SSM_GROUPS, SSM_GROUP)
    uc = u_ctx.astype(F32).reshape(B, u_ctx.shape[1], SSM_GROUPS, SSM_GROUP)
    y_lat = 0.0
    y_ctx = 0.0
    for d in range(2):
        disc = s5_discretise(lam_re[d], lam_im[d], log_dt[d], b_re[d], b_im[d])
        cr = c_re[d].astype(F32)
        ci = c_im[d].astype(F32)
        ucd, uld = (uc, ul) if d == 0 else (uc[:, ::-1], ul[:, ::-1])
        sc = s5_states(ucd, disc, None)
        sl = s5_states(uld, disc, (sc[0][:, -1], sc[1][:, -1]))
        yl = s5_readout(sl, cr, ci)
        y_lat = y_lat + (yl if d == 0 else yl[:, ::-1])
        if need_ctx:
            yc = s5_readout(sc, cr, ci)
            y_ctx = y_ctx + (yc if d == 0 else yc[:, ::-1])
    out_lat = s5_output(y_lat, u_lat, d_skip, w_glu, b_glu)
    out_ctx = s5_output(y_ctx, u_ctx, d_skip, w_glu, b_glu) if need_ctx else None
    return out_lat, out_ctx


def neighbourhood_attention(q, k, v, k_ctx, v_ctx, rpb):
    B, S, H, dh = q.shape
    rows = S // GRID_W
    wr = min(NA_WIN_ROWS, rows)
    wc = NA_WIN_COLS
    r = jnp.arange(rows)
    key_rows = jnp.clip(r - wr // 2, 0, rows - wr)[:, None] + jnp.arange(wr)[None, :]
    n_lat = wr * GRID_W
    qr = q.reshape(B, rows, GRID_W, H, dh)
    kg = k.reshape(B, rows, GRID_W, H, dh)[:, key_rows].reshape(B, rows, n_lat, H, dh)
    vg = v.reshape(B, rows, GRID_W, H, dh)[:, key_rows].reshape(B, rows, n_lat, H, dh)
    col = jnp.arange(GRID_W)
    col_start = jnp.clip(col - wc // 2, 0, GRID_W - wc)
    in_win = (col[None, :] >= col_start[:, None]) & (col[None, :] < col_start[:, None] + wc)
    mask = jnp.broadcast_to(in_win[:, None, :], (GRID_W, wr, GRID_W)).reshape(GRID_W, n_lat)
    dr = key_rows - r[:, None]
    dc = col[None, :] - col[:, None]
    idx_r = (dr + NA_WIN_ROWS - 1)[:, None, :, None]
    idx_c = (jnp.clip(dc, -(wc - 1), wc - 1) + NA_WIN_COLS - 1)[None, :, None, :]
    bias = rpb[:, idx_r, idx_c].reshape(H, rows, GRID_W, n_lat).astype(F32)
    s_lat = jnp.einsum('brqhd,brkhd->bhrqk', qr, kg).astype(F32) * NA_SCALE + bias[None]
    s_lat = jnp.where(mask, s_lat, NEG_INF)
    s_ctx = jnp.einsum('brqhd,bkhd->bhrqk', qr, k_ctx).astype(F32) * NA_SCALE
    p = jax.nn.softmax(jnp.concatenate([s_lat, s_ctx], axis=-1), axis=-1).astype(v.dtype)
    o = (jnp.einsum('bhrqk,brkhd->brqhd', p[..., :n_lat], vg)
         + jnp.einsum('bhrqk,bkhd->brqhd', p[..., n_lat:], v_ctx))
    return o.reshape(B, S, H * dh)


def mla_queries(c_q, q_norm, w_uq, rope):
    B, L, _ = c_q.shape
    q = (rms_norm(c_q, q_norm) @ w_uq).reshape(B, L, MLA_HEADS, MLA_QK_DIM)
    q_nope = q[..., :MLA_NOPE_DIM]
    q_pe = q[..., MLA_NOPE_DIM:]
    if rope is not None:
        q_pe = apply_axial_rope(q_pe, rope)
    return jnp.concatenate([q_nope, q_pe], axis=-1)


def mla_keys_values(c_kv, k_rope, kv_norm, w_ukv, rope):
    B, L, _ = c_kv.shape
    kv = (rms_norm(c_kv, kv_norm) @ w_ukv).reshape(B, L, MLA_HEADS, MLA_NOPE_DIM + MLA_V_DIM)
    k_nope = kv[..., :MLA_NOPE_DIM]
    v = kv[..., MLA_NOPE_DIM:]
    k_pe = k_rope[:, :, None, :]
    if rope is not None:
        k_pe = apply_axial_rope(k_pe, rope)
    k = jnp.concatenate([k_nope, jnp.broadcast_to(k_pe, (B, L, MLA_HEADS, MLA_ROPE_DIM))], axis=-1)
    return k, v


def merge_branches(ya, za, yn, zn, ym, zm, ga, gn, gm, w_branch_a, w_branch_b, w_branch_c, w_out):
    silu = jax.nn.silu
    sig = jax.nn.sigmoid
    m = (sig(ga) * ((ya * silu(za)) @ w_branch_a)
         + sig(gn) * ((yn * silu(zn)) @ w_branch_b)
         + sig(gm) * ((ym * silu(zm)) @ w_branch_c))
    return m @ w_out


def na_heads(t):
    return t.reshape(t.shape[0], t.shape[1], NA_HEADS, NA_HEAD_DIM)


def trunk_layer(x, ctx, mod_lat, mod_ctx, w_in, lam_re, lam_im, log_dt, b_re, b_im, c_re, c_im,
                d_skip, w_glu, b_glu, rpb, q_norm, w_uq, kv_norm, w_ukv,
                w_branch_a, w_branch_b, w_branch_c, w_out, ln_g, ln_b, rope, need_ctx):
    shift, scale, gate = jnp.split(mod_lat, 3, axis=-1)
    shift_c, scale_c, gate_c = jnp.split(mod_ctx, 3, axis=-1)
    h = layer_norm(x) * (1.0 + scale[:, None]) + shift[:, None]
    hc = layer_norm(ctx) * (1.0 + scale_c) + shift_c
    (ua, za, qn, kn, vn, zn, cq, ckv, kr, zm, ga, gn, gm) = split_cols(h @ w_in)
    (ua_c, za_c, qn_c, kn_c, vn_c, zn_c, cq_c, ckv_c, kr_c, zm_c, ga_c, gn_c, gm_c) = split_cols(hc @ w_in)

    ya, ya_c = s5_mixer(ua, ua_c, lam_re, lam_im, log_dt, b_re, b_im, c_re, c_im,
                        d_skip, w_glu, b_glu, need_ctx)
    kn_ch = na_heads(kn_c)
    vn_ch = na_heads(vn_c)
    yn = neighbourhood_attention(na_heads(qn), na_heads(kn), na_heads(vn), kn_ch, vn_ch, rpb)
    q_l = mla_queries(cq, q_norm, w_uq, rope)
    k_l, v_l = mla_keys_values(ckv, kr, kv_norm, w_ukv, rope)
    k_c, v_c = mla_keys_values(ckv_c, kr_c, kv_norm, w_ukv, None)
    ym = block_attention(q_l, jnp.concatenate([k_l, k_c], axis=1),
                         jnp.concatenate([v_l, v_c], axis=1), MLA_SCALE)

    out = merge_branches(ya, za, yn, zn, ym, zm, ga, gn, gm, w_branch_a, w_branch_b, w_branch_c, w_out)
    x_new = layer_norm(DEEPNORM_ALPHA * x + gate[:, None] * out, ln_g, ln_b)
    if not need_ctx:
        return x_new, None

    yn_c = block_attention(na_heads(qn_c), kn_ch, vn_ch, NA_SCALE)
    q_c = mla_queries(cq_c, q_norm, w_uq, None)
    ym_c = block_attention(q_c, k_c, v_c, MLA_SCALE)
    out_c = merge_branches(ya_c, za_c, yn_c, zn_c, ym_c, zm_c, ga_c, gn_c, gm_c,
                           w_branch_a, w_branch_b, w_branch_c, w_out)
    ctx_new = layer_norm(DEEPNORM_ALPHA * ctx + gate_c * out_c, ln_g, ln_b)
    return x_new, ctx_new


def setup_inputs(seed: int = 0) -> dict:
    key = jax.random.key(seed)
    ks = jax.random.split(key, 32)
    D = D_MODEL
    G = SSM_GROUPS
    P = SSM_STATE
    Gi = SSM_GROUP

    def nrm(k, shape, s):
        return jax.random.normal(k, shape, F32) * s

    return {
        'x': nrm(ks[0], (BATCH, SEQ, D), 1.0),
        'c': nrm(ks[1], (BATCH, D), 1.0),
        'ctx': nrm(ks[2], (BATCH, CTX_LEN, D), 1.0),
        'c_ctx': nrm(ks[3], (D,), 1.0),
        'w_mod': nrm(ks[4], (DEPTH, D, 3 * D), D ** -0.5),
        'b_mod': nrm(ks[5], (DEPTH, 3 * D), 0.02),
        'w_in': nrm(ks[6], (DEPTH, D, IN_WIDTH), D ** -0.5),
        'ssm_lam_re': -0.5 + nrm(ks[7], (DEPTH, 2, G, P), 0.01),
        'ssm_lam_im': math.pi * jnp.arange(P, dtype=F32) + nrm(ks[8], (DEPTH, 2, G, P), 0.01),
        'ssm_log_dt': jax.random.uniform(ks[9], (DEPTH, 2, G), F32,
                                         math.log(SSM_DT_MIN), math.log(SSM_DT_MAX)),
        'ssm_b_re': nrm(ks[10], (DEPTH, 2, G, P, Gi), (2 * Gi) ** -0.5),
        'ssm_b_im': nrm(ks[11], (DEPTH, 2, G, P, Gi), (2 * Gi) ** -0.5),
        'ssm_c_re': nrm(ks[12], (DEPTH, 2, G, Gi, P), 0.5),
        'ssm_c_im': nrm(ks[13], (DEPTH, 2, G, Gi, P), 0.5),
        'ssm_d': nrm(ks[14], (DEPTH, SSM_WIDTH), 1.0),
        'ssm_w_glu': nrm(ks[15], (DEPTH, SSM_WIDTH, SSM_WIDTH), SSM_WIDTH ** -0.5),
        'ssm_b_glu': nrm(ks[16], (DEPTH, SSM_WIDTH), 0.02),
        'na_rpb': nrm(ks[17], (DEPTH, NA_HEADS, 2 * NA_WIN_ROWS - 1, 2 * NA_WIN_COLS - 1), 0.02),
        'mla_q_norm': 1.0 + nrm(ks[18], (DEPTH, MLA_Q_RANK), 0.02),
        'mla_w_uq': nrm(ks[19], (DEPTH, MLA_Q_RANK, MLA_HEADS * MLA_QK_DIM), MLA_Q_RANK ** -0.5),
        'mla_kv_norm': 1.0 + nrm(ks[20], (DEPTH, MLA_KV_RANK), 0.02),
        'mla_w_ukv': nrm(ks[21], (DEPTH, MLA_KV_RANK, MLA_HEADS * (MLA_NOPE_DIM + MLA_V_DIM)),
                         MLA_KV_RANK ** -0.5),
        'w_branch_a': nrm(ks[22], (DEPTH, SSM_WIDTH, D), SSM_WIDTH ** -0.5),
        'w_branch_b': nrm(ks[23], (DEPTH, NA_WIDTH, D), NA_WIDTH ** -0.5),
        'w_branch_c': nrm(ks[24], (DEPTH, MLA_WIDTH, D), MLA_WIDTH ** -0.5),
        'w_out': nrm(ks[25], (DEPTH, D, D), DEEPNORM_BETA * D ** -0.5),
        'ln_g': 1.0 + nrm(ks[26], (DEPTH, D), 0.02),
        'ln_b': nrm(ks[27], (DEPTH, D), 0.02),
    }


def reference(x, c, ctx, c_ctx, w_mod, b_mod, w_in, ssm_lam_re, ssm_lam_im, ssm_log_dt,
              ssm_b_re, ssm_b_im, ssm_c_re, ssm_c_im, ssm_d, ssm_w_glu, ssm_b_glu, na_rpb,
              mla_q_norm, mla_w_uq, mla_kv_norm, mla_w_ukv, w_branch_a, w_branch_b, w_branch_c,
              w_out, ln_g, ln_b):
    rope = axial_rope_tables(x.shape[1])
    sc = jax.nn.silu(c)
    scc = jax.nn.silu(c_ctx)
    for l in range(DEPTH):
        need_ctx = l < DEPTH - 1
        mod_lat = sc @ w_mod[l] + b_mod[l]
        mod_ctx = scc @ w_mod[l] + b_mod[l]
        x, ctx = trunk_layer(x, ctx, mod_lat, mod_ctx, w_in[l],
                             ssm_lam_re[l], ssm_lam_im[l], ssm_log_dt[l],
                             ssm_b_re[l], ssm_b_im[l], ssm_c_re[l], ssm_c_im[l],
                             ssm_d[l], ssm_w_glu[l], ssm_b_glu[l], na_rpb[l],
                             mla_q_norm[l], mla_w_uq[l], mla_kv_norm[l], mla_w_ukv[l],
                             w_branch_a[l], w_branch_b[l], w_branch_c[l], w_out[l],
                             ln_g[l], ln_b[l], rope, need_ctx)
    return x
```

```python
import numpy as np
import concourse.bass as bass
import concourse.mybir as mybir
from concourse.bass_utils import run_bass_kernel_spmd

F32 = mybir.dt.float32
BF16 = mybir.dt.bfloat16
ALU = mybir.AluOpType
AF = mybir.ActivationFunctionType
AX = mybir.AxisListType


class Res:
    __slots__ = ("name", "last_w", "readers")

    def __init__(self, name):
        self.name = name
        self.last_w = None
        self.readers = {}


class Tile:
    def __init__(self, h, name):
        self.h = h
        self.res = Res(name)

    def __getitem__(self, k):
        return self.h[k]


def _res(x):
    return x.res if isinstance(x, Tile) else x


class Prog:
    ENG = ("pe", "act", "dve", "pool", "sp")
    RING = 16

    def __init__(self):
        self.nc = bass.Bass("TRN2", target_bir_lowering=False)
        self.q = {e: [] for e in self.ENG}
        self.cnt = {e: 0 for e in self.ENG}
        self.dman = {e: 0 for e in self.ENG}
        self.waited = {e: {} for e in self.ENG}
        self.sems = {}
        self.ntile = 0

    def eng(self, e):
        nc = self.nc
        return {"pe": nc.tensor, "act": nc.scalar, "dve": nc.vector, "pool": nc.gpsimd, "sp": nc.sync}[e]

    def sbuf(self, name, shape, dtype=F32):
        self.ntile += 1
        return Tile(self.nc.alloc_sbuf_tensor(f"{name}_{self.ntile}", list(shape), dtype), name)

    def psum(self, name, shape, dtype=F32):
        self.ntile += 1
        return Tile(self.nc.alloc_psum_tensor(f"{name}_{self.ntile}", list(shape), dtype), name)

    def dram_in(self, name, shape, dtype=F32):
        return Tile(self.nc.dram_tensor(name, list(shape), dtype, kind="ExternalInput"), name)

    def dram_out(self, name, shape, dtype=F32):
        return Tile(self.nc.dram_tensor(name, list(shape), dtype, kind="ExternalOutput"), name)

    def dram_tmp(self, name, shape, dtype=F32):
        return Tile(self.nc.dram_tensor(name, list(shape), dtype, kind="Internal"), name)

    def op(self, e, fn, reads=(), writes=(), dma=False):
        toks = {}

        def add(t):
            if t is None:
                return
            k, v = t
            if toks.get(k, 0) < v:
                toks[k] = v

        reads = [_res(r) for r in reads]
        writes = [_res(w) for w in writes]
        for r in reads:
            add(r.last_w)
        for w in writes:
            add(w.last_w)
            for k, v in w.readers.items():
                add((k, v))
        if fn is None:
            tok = None
        elif dma:
            n = self.dman[e]
            self.dman[e] += 1
            slot, use = n % self.RING, n // self.RING
            key = ("d", e, slot)
            if use > 0:
                add((key, 16 * use))
            tok = (key, 16 * (use + 1))
        else:
            self.cnt[e] += 1
            tok = (("c", e), self.cnt[e])
        waits = []
        wd = self.waited[e]
        for k, v in toks.items():
            if e == "pe" and k == ("c", "pe"):
                continue
            if wd.get(k, 0) >= v:
                continue
            wd[k] = v
            waits.append((k, v))
        self.q[e].append((waits, fn, tok))
        if tok is not None:
            for r in reads:
                if r.readers.get(tok[0], 0) < tok[1]:
                    r.readers[tok[0]] = tok[1]
            for w in writes:
                w.last_w = tok
                w.readers = {}
        return tok

    def fence(self, e, res):
        self.op(e, None, reads=res, writes=res)

    def _sem(self, key):
        if key not in self.sems:
            self.sems[key] = self.nc.alloc_semaphore(name="s_" + "_".join(str(x) for x in key))
        return self.sems[key]

    def emit(self):
        nc = self.nc
        for e in self.ENG:
            for waits, fn, tok in self.q[e]:
                for k, v in waits:
                    self._sem(k)
                if tok is not None:
                    self._sem(tok[0])

        def replay(e):
            eng = self.eng(e)
            for waits, fn, tok in self.q[e]:
                for k, v in waits:
                    eng.wait_ge(self.sems[k], v)
                if fn is not None:
                    ins = fn()
                    ins.then_inc(self.sems[tok[0]], 16 if tok[0][0] == "d" else 1)

        with nc.Block() as block:
            @block.tensor
            def _(x):
                replay("pe")

            @block.scalar
            def _(x):
                replay("act")

            @block.vector
            def _(x):
                replay("dve")

            @block.gpsimd
            def _(x):
                replay("pool")

            @block.sync
            def _(x):
                replay("sp")
        return nc

    def dma(self, e, out, in_, reads, writes, **kw):
        eng = self.eng(e)
        self.op(e, lambda: eng.dma_start(out=out, in_=in_, **kw), reads, writes, dma=True)

    def mm(self, out, lhsT, rhs, start, stop, reads, writes):
        nc = self.nc
        self.op("pe", lambda: nc.tensor.matmul(out, lhsT, rhs, start=start, stop=stop), reads, writes)

    def tr(self, out, in_, ident, reads, writes):
        nc = self.nc
        self.op("pe", lambda: nc.tensor.transpose(out, in_, ident), reads, writes)

    def act(self, out, in_, func, reads, writes, bias=None, scale=None, accum_out=None, e="act"):
        nc = self.nc
        kw = {}
        if bias is not None:
            kw["bias"] = bias
        if scale is not None:
            kw["scale"] = scale
        if accum_out is not None:
            kw["accum_out"] = accum_out
        self.op("act", lambda: nc.scalar.activation(out, in_, func, **kw), reads, writes)

    def ts(self, e, out, in0, s1, s2, op0, op1, reads, writes, accum_out=None):
        eng = self.eng(e)
        if op1 is None:
            self.op(e, lambda: eng.tensor_scalar(out, in0, s1, None, op0), reads, writes)
        elif accum_out is not None:
            self.op(e, lambda: eng.tensor_scalar(out, in0, s1, s2, op0, op1, accum_out), reads, writes)
        else:
            self.op(e, lambda: eng.tensor_scalar(out, in0, s1, s2, op0, op1), reads, writes)

    def tt(self, e, out, in0, in1, op, reads, writes):
        eng = self.eng(e)
        self.op(e, lambda: eng.tensor_tensor(out, in0, in1, op), reads, writes)

    def stt(self, e, out, in0, scalar, in1, op0, op1, reads, writes):
        eng = self.eng(e)
        self.op(e, lambda: eng.scalar_tensor_tensor(out, in0, scalar, in1, op0, op1), reads, writes)

    def copy(self, e, out, in_, reads, writes):
        eng = self.eng(e)
        if e == "act":
            self.op(e, lambda: eng.copy(out, in_), reads, writes)
        else:
            self.op(e, lambda: eng.tensor_copy(out, in_), reads, writes)

    def memset(self, e, ap, val, writes):
        eng = self.eng(e)
        self.op(e, lambda: eng.memset(ap, val), (), writes)


def run(P, in_maps, trace=False):
    return run_bass_kernel_spmd(P.nc, in_maps, core_ids=list(range(len(in_maps))), trace=trace)


import ml_dtypes
BF = ml_dtypes.bfloat16
SWAP = np.concatenate([np.arange(8, 16), np.arange(0, 8), np.arange(24, 32), np.arange(16, 24)])

def rope_tables():
    t = np.arange(8192)
    row = (t // 64).astype(np.float32); col = (t % 64).astype(np.float32)
    inv = (1.0 / (np.float32(10000.0) ** (np.arange(0, 16, 2, dtype=np.float32) / np.float32(16)))).astype(np.float32)
    ar = row[:, None] * inv[None, :]; ac = col[:, None] * inv[None, :]
    cr, sr, cc, sn = np.cos(ar), np.sin(ar), np.cos(ac), np.sin(ac)
    C32 = np.concatenate([cr, cr, cc, cc], axis=1).T.astype(np.float32)
    S32 = np.concatenate([-sr, sr, -sn, sn], axis=1).T.astype(np.float32)
    return C32, S32

def l1_inputs(inp, l, x_cur, ctx_cur):
    C32, S32 = rope_tables()
    w_in = inp['w_in'][l]
    w_uq = inp['mla_w_uq'][l]
    w_uq2 = w_uq.reshape(256, 8, 96).copy()
    w_uq2[:, :, 64:] = w_uq2[:, :, 64:][:, :, SWAP]
    common = dict(
        w_mod=inp['w_mod'][l], b_mod=np.ascontiguousarray(inp['b_mod'][l].reshape(24, 128).T),
        w_in=w_in, w_krs=np.ascontiguousarray(w_in[:, 3456 + SWAP]),
        qn=np.ascontiguousarray(inp['mla_q_norm'][l].reshape(2, 128).T), kvn=np.ascontiguousarray(inp['mla_kv_norm'][l].reshape(1, 128).T),
        w_uq=w_uq, w_uq2=np.ascontiguousarray(w_uq2.reshape(256, 768)), w_ukv=inp['mla_w_ukv'][l],
        ident=np.eye(128, dtype=np.float32))
    maps = []
    for c in range(8):
        b, j = c // 4, c % 4
        x = np.concatenate([x_cur[b, j * 2048:(j + 1) * 2048], ctx_cur[b]], axis=0)
        csc = np.zeros((128, 8, 2), np.float32)
        csc[:, :, 0] = inp['c'][b].reshape(8, 128).T
        csc[:, :, 1] = inp['c_ctx'].reshape(8, 128).T
        c32 = np.concatenate([C32[:, j * 2048:(j + 1) * 2048], np.ones((32, 256), np.float32)], axis=1)
        s32 = np.concatenate([S32[:, j * 2048:(j + 1) * 2048], np.zeros((32, 256), np.float32)], axis=1)
        c96 = np.concatenate([np.ones((64, 2304), np.float32), c32], axis=0)
        s96 = np.concatenate([np.zeros((64, 2304), np.float32), s32], axis=0)
        m = dict(common); m.update(x=np.ascontiguousarray(x), csc=csc.reshape(128, 16), c96=c96, s96=s96, c32=c32, s32=s32)
        maps.append(m)
    return maps

def s5_inputs(inp, l, ua_lat, ua_ctx):
    B = ua_lat.shape[0]
    seqs = [[np.concatenate([ua_ctx[b], ua_lat[b]], 0) for b in range(B)],
            [np.concatenate([ua_ctx[b][::-1], ua_lat[b][::-1]], 0) for b in range(B)]]
    tm = np.zeros((8, 16, 8, 16), np.float32)
    for sp in range(8):
        for t in range(8):
            if t >= 7 - sp:
                tm[sp, :, t, :] = 1.0
    tm = tm.reshape(128, 128)
    maps = []
    for c in range(8):
        u = np.empty((4, 2, 2, 128, 1056), np.float32)
        par = np.empty((128, 12), np.float32)
        bc = np.empty((128, 4, 4, 16), np.float32)
        for q in range(4):
            g = 4 * c + q
            for d in range(2):
                sl = slice(d * 64, d * 64 + 64)
                par[sl, q] = inp['ssm_lam_re'][l, d, g]
                par[sl, 4 + q] = inp['ssm_lam_im'][l, d, g]
                par[sl, 8 + q] = inp['ssm_log_dt'][l, d, g]
                bc[sl, q, 0] = inp['ssm_b_re'][l, d, g]
                bc[sl, q, 1] = inp['ssm_b_im'][l, d, g]
                bc[sl, q, 2] = inp['ssm_c_re'][l, d, g].T
                bc[sl, q, 3] = inp['ssm_c_im'][l, d, g].T
                for b in range(B):
                    X = seqs[d][b][:, 16 * g:16 * g + 16].reshape(1056, 8, 16)[:, ::-1, :]
                    u[q, d, b] = X.transpose(1, 2, 0).reshape(128, 1056)
        maps.append(dict(u=u, par=par, bc=bc, tmask=tm, ident=np.eye(128, dtype=np.float32)))
    return maps

def s5_gather(results, B=2):
    yf = np.empty((B, 8448, 512), np.float32); yb = np.empty((B, 8448, 512), np.float32)
    for c in range(8):
        y = results[c]["y"]
        for q in range(4):
            g = 4 * c + q
            for b in range(B):
                for d in range(2):
                    Y = y[q, d, b].reshape(8, 16, 1056).transpose(2, 0, 1).reshape(8448, 16)
                    if d == 0:
                        yf[b, :, 16 * g:16 * g + 16] = Y
                    else:
                        yb[b, :256, 16 * g:16 * g + 16] = Y[:256][::-1]
                        yb[b, 256:, 16 * g:16 * g + 16] = Y[256:][::-1]
    return yf, yb

NEG = -30000.0

def _u16(a):
    return np.ascontiguousarray(a)

def attn_consts(rpb_l):
    kc = np.arange(64)[:, None]; qc = np.arange(64)[None, :]
    cs = np.clip(qc - 8, 0, 48)
    colvalid = (kc >= cs) & (kc < cs + 16)
    ci = np.clip(kc - qc, -15, 15) + 15
    gp = np.full((8, 2, 64, 22, 64), NEG, np.float32)
    for kl in range(2):
        for e in range(22):
            dr = kl + 10 - e
            if abs(dr) <= 7:
                vals = rpb_l[:, dr + 7, :][:, ci]
                gp[:, kl, :, e, :] = np.where(colvalid[None], vals, np.float32(NEG))
    gp = gp.reshape(8, 128, 22 * 64)
    bq = np.zeros((8, 8, 64), np.float32)
    for r in range(8):
        bq[r, r, :] = 1.0
    return gp, bq.reshape(8, 512).astype(BF)

def attn_rowmask(j):
    am = np.full((8, 4, 8, 2, 64), NEG, np.float32)
    for blk in range(4):
        for ql in range(8):
            qr = 32 * j + 8 * blk + ql
            ws = int(np.clip(qr - 4, 0, 120))
            for i in range(8):
                for kl in range(2):
                    kr = 32 * j + 8 * blk - 4 + 2 * i + kl
                    if 0 <= kr < 128 and ws <= kr < ws + 8:
                        am[ql, blk, i, kl, :] = 0.0
    return am.reshape(8, 32 * 128).astype(BF)

def attn_inputs(inp, l, r1):
    gp, bq = attn_consts(inp['na_rpb'][l])
    one = np.ones((), BF)
    maps = []
    per_b = {}
    for b in range(2):
        cores = [r1[4 * b + j] for j in range(4)]
        klat = np.concatenate([c['mknT'][:, :2048] for c in cores], axis=1).reshape(8, 64, 8192)
        kctx = cores[0]['mknT'][:, 2048:].reshape(8, 64, 256)
        kpe = np.concatenate([c['mkpe'][:, :2048] for c in cores] + [cores[0]['mkpe'][:, 2048:]], axis=1)
        mk = np.concatenate([np.concatenate([klat, kctx], axis=2), np.broadcast_to(kpe[None], (8, 32, 8448))], axis=1)
        v = np.concatenate([c['mv'][:2048] for c in cores] + [cores[0]['mv'][2048:]], axis=0).reshape(8448, 8, 64)
        va = np.concatenate([v, np.ones((8448, 8, 1), BF)], axis=2)
        mv = va.reshape(66, 128, 8, 65).transpose(2, 1, 0, 3).reshape(8, 128, 66 * 65)
        nkl = np.concatenate([c['nakT'][:, :2048] for c in cores], axis=1).reshape(8, 64, 8192)
        nkc = cores[0]['nakT'][:, 2048:].reshape(8, 64, 256)
        nvl = np.concatenate([c['nav'][:2048] for c in cores], axis=0)
        nvc = cores[0]['nav'][2048:]
        per_b[b] = (np.ascontiguousarray(mk), np.ascontiguousarray(mv), nkl, nkc, nvl, nvc)
    for c in range(8):
        b, j = c // 4, c % 4
        mk, mv, nkl, nkc, nvl, nvc = per_b[b]
        t0 = (32 * j - 4) * 64
        nk = np.zeros((8, 64, 2816), BF); nvv = np.zeros((2816, 512), BF)
        lo, hi = max(t0, 0), min(t0 + 2560, 8192)
        nk[:, :, lo - t0:hi - t0] = nkl[:, :, lo:hi]
        nk[:, :, 2560:] = nkc
        nvv[lo - t0:hi - t0] = nvl[lo:hi]
        nvv[2560:] = nvc
        nva = np.concatenate([nvv.reshape(2816, 8, 64), np.ones((2816, 8, 1), BF)], axis=2)
        nv = nva.reshape(22, 128, 8, 65).transpose(2, 1, 0, 3).reshape(8, 128, 22 * 65)
        maps.append(dict(mq=r1[c]['mqT'], mk=mk, mv=mv, nq=np.ascontiguousarray(r1[c]['naqT'].reshape(8, 64, 2304)),
                         nk=nk, nv=np.ascontiguousarray(nv), gp=gp, am=attn_rowmask(j), bq=bq))
    return maps

def l3_inputs(inp, l, x_cur, ctx_cur, r1, yf, yb, r3):
    bc = lambda v: np.ascontiguousarray(np.broadcast_to(v[None, :], (128, v.shape[0])))
    pl = lambda v: np.ascontiguousarray(v.reshape(-1, 128).T)
    common = dict(dskip=pl(inp['ssm_d'][l]), bglu=pl(inp['ssm_b_glu'][l]), w_glu=inp['ssm_w_glu'][l],
                  w_a=inp['w_branch_a'][l], w_b=inp['w_branch_b'][l], w_c=inp['w_branch_c'][l], w_o=inp['w_out'][l],
                  w_gate=np.ascontiguousarray(inp['w_mod'][l][:, 2048:3072]), b_gate=bc(inp['b_mod'][l][2048:3072]),
                  ln_g=bc(inp['ln_g'][l]), ln_b=bc(inp['ln_b'][l]))
    maps = []
    for c in range(8):
        b, j = c // 4, c % 4
        x = np.concatenate([x_cur[b, j * 2048:(j + 1) * 2048], ctx_cur[b]], axis=0)
        csc = np.zeros((128, 8, 2), np.float32)
        csc[:, :, 0] = inp['c'][b].reshape(8, 128).T
        csc[:, :, 1] = inp['c_ctx'].reshape(8, 128).T
        own = lambda y: np.ascontiguousarray(np.concatenate([y[b, 256 + j * 2048:256 + (j + 1) * 2048], y[b, :256]], axis=0).T)
        m = dict(common)
        m.update(x=np.ascontiguousarray(x), csc=csc.reshape(128, 16), yfT=own(yf), ybT=own(yb), uaT=r1[c]['uaT'], zT=r1[c]['zT'],
                 gT=r1[c]['gT'], ynT=r3[c]['ynT'], ymT=r3[c]['ymT'])
        maps.append(m)
    return maps

def ua_from_l1(r1):
    ua_lat = np.empty((2, 8192, 512), np.float32); ua_ctx = np.empty((2, 256, 512), np.float32)
    for c in range(8):
        b, j = c // 4, c % 4
        ua_lat[b, j * 2048:(j + 1) * 2048] = r1[c]['uaT'][:, :2048].T
        if j == 0:
            ua_ctx[b] = r1[c]['uaT'][:, 2048:].T
    return ua_lat, ua_ctx


NTOK = 2304
NLAT = 2048
TB = [(0, 512), (512, 512), (1024, 512), (1536, 512), (2048, 256)]
LN_EPS = 1e-5
RMS_EPS = 1e-6


def build_l1():
    P = Prog()
    nc = P.nc
    x_d = P.dram_in("x", [NTOK, 1024])
    csc_d = P.dram_in("csc", [128, 16])
    wmod_d = P.dram_in("w_mod", [1024, 3072])
    bmod_d = P.dram_in("b_mod", [128, 24])
    win_d = P.dram_in("w_in", [1024, 7072])
    wkrs_d = P.dram_in("w_krs", [1024, 32])
    qn_d = P.dram_in("qn", [128, 2])
    kvn_d = P.dram_in("kvn", [128, 1])
    wuq_d = P.dram_in("w_uq", [256, 768])
    wuq2_d = P.dram_in("w_uq2", [256, 768])
    wukv_d = P.dram_in("w_ukv", [128, 1024])
    c96_d = P.dram_in("c96", [96, NTOK])
    s96_d = P.dram_in("s96", [96, NTOK])
    c32_d = P.dram_in("c32", [32, NTOK])
    s32_d = P.dram_in("s32", [32, NTOK])
    idd = P.dram_in("ident", [128, 128])

    uaT_o = P.dram_out("uaT", [512, NTOK])
    zT_o = P.dram_out("zT", [1536, NTOK])
    g3_o = P.dram_out("gT", [3072, NTOK])
    naq_o = P.dram_out("naqT", [512, NTOK], BF16)
    nak_o = P.dram_out("nakT", [512, NTOK], BF16)
    nav_o = P.dram_out("nav", [NTOK, 512], BF16)
    mq_o = P.dram_out("mqT", [8, 96, NTOK], BF16)
    mkn_o = P.dram_out("mknT", [512, NTOK], BF16)
    mkpe_o = P.dram_out("mkpe", [32, NTOK], BF16)
    mv_o = P.dram_out("mv", [NTOK, 512], BF16)
    outs = [uaT_o, zT_o, g3_o, naq_o, nak_o, nav_o, mq_o, mkn_o, mkpe_o, mv_o]

    identf = P.sbuf("identf", [128, 128])
    ident = P.sbuf("ident", [128, 128], BF16)
    ones = P.sbuf("ones", [128, 128], BF16)
    hT = P.sbuf("hT", [128, 8, NTOK], BF16)
    wst = [P.sbuf(f"wst{i}", [128, 8, 512]) for i in range(2)]
    wbf = [P.sbuf(f"wbf{i}", [128, 8, 512], BF16) for i in range(2)]
    csc = P.sbuf("csc", [128, 16])
    sc = P.sbuf("sc", [128, 16])
    bmod = P.sbuf("bmod", [128, 24])
    mods = P.sbuf("mods", [128, 16, 2])
    xt = [P.sbuf(f"xt{i}", [128, 1024]) for i in range(2)]
    xh = [P.sbuf(f"xh{i}", [128, 1024], BF16) for i in range(2)]
    st = [P.sbuf(f"st{i}", [128, 2, 6]) for i in range(2)]
    mv = [P.sbuf(f"mv{i}", [128, 4]) for i in range(2)]
    cmst = [P.sbuf(f"cmst{i}", [128, NTOK]) for i in range(2)]
    cmsb = [P.sbuf(f"cmsb{i}", [128, NTOK], BF16) for i in range(2)]
    tmst = [P.sbuf(f"tmst{i}", [128, 512]) for i in range(2)]
    tmsb = [P.sbuf(f"tmsb{i}", [128, 512], BF16) for i in range(2)]
    psA = [P.psum(f"psA{i}", [128, 512]) for i in range(4)]
    psT = [P.psum(f"psT{i}", [128, 1024], BF16) for i in range(2)]
    psM = P.psum("psM", [128, 512])
    psB = P.psum("psB", [128, 512])
    rr = {"ps": 0, "ev": 0}

    def next_ps():
        rr["ps"] += 1
        return psA[rr["ps"] % 4]

    def evac(out, in_, reads, writes):
        rr["ev"] += 1
        if rr["ev"] % 2:
            P.copy("act", out, in_, reads, writes)
        else:
            P.copy("dve", out, in_, reads, writes)

    P.dma("sp", identf[:], idd[:], [idd], [identf])
    P.copy("dve", ident[:], identf[:], [identf], [ident])
    P.memset("pool", ones[:], 1.0, [ones])
    P.dma("sp", csc[:], csc_d[:], [csc_d], [csc])
    P.dma("sp", bmod[:], bmod_d[:], [bmod_d], [bmod])
    P.act(sc[:], csc[:], AF.Silu, [csc], [sc])

    for ch in range(4):
        w = wst[ch % 2]
        P.dma("sp", w[:], wmod_d[:, ch * 512:(ch + 1) * 512].rearrange("(k p) n -> p k n", p=128), [wmod_d], [w])
        for j in range(4):
            f = ch * 4 + j
            for k in range(8):
                P.mm(psM[:, 2 * f:2 * f + 2], w[:, k, j * 128:(j + 1) * 128], sc[:, 2 * k:2 * k + 2],
                     k == 0, k == 7, [w, sc], [psM])
    P.tt("dve", mods[:], psM[:, 0:32].rearrange("p (f c) -> p f c", c=2),
         bmod[:, 0:16].unsqueeze(2).to_broadcast([128, 16, 2]), ALU.add, [psM, bmod], [mods])
    P.ts("dve", mods[:, 8:16, :], mods[:, 8:16, :], 1.0, None, ALU.add, None, [mods], [mods])

    for t in range(18):
        X, XH, ST, MV = xt[t % 2], xh[t % 2], st[t % 2], mv[t % 2]
        P.dma("sp", X[:], x_d[t * 128:(t + 1) * 128, :], [x_d], [X])
        P.op("dve", lambda ST=ST, X=X: nc.vector.bn_stats(ST[:, 0, :], X[:, 0:512]), [X], [ST])
        P.op("dve", lambda ST=ST, X=X: nc.vector.bn_stats(ST[:, 1, :], X[:, 512:1024]), [X], [ST])
        P.op("dve", lambda ST=ST, MV=MV: nc.vector.bn_aggr(MV[:, 0:2], ST[:]), [ST], [MV])
        P.act(MV[:, 2:3], MV[:, 1:2], AF.Sqrt, [MV], [MV], bias=LN_EPS, scale=1.0)
        P.op("dve", lambda MV=MV: nc.vector.reciprocal(MV[:, 3:4], MV[:, 2:3]), [MV], [MV])
        P.ts("dve", XH[:], X[:], MV[:, 0:1], MV[:, 3:4], ALU.subtract, ALU.mult, [X, MV], [XH])
        pt = psT[t % 2]
        for k in range(8):
            P.tr(pt[:, k * 128:(k + 1) * 128], XH[:, k * 128:(k + 1) * 128], ident[:], [XH, ident], [pt])
        c = 0 if t < 16 else 1
        for k in range(8):
            P.act(hT[:, k, t * 128:(t + 1) * 128], pt[:, k * 128:(k + 1) * 128], AF.Identity, [pt, mods], [hT],
                  bias=mods[:, k, c:c + 1], scale=mods[:, 8 + k, c:c + 1])

    wn = {"n": 0}

    def load_w(src_ap, src_res, width):
        i = wn["n"] % 2
        wn["n"] += 1
        P.dma("sp", wst[i][:, :, 0:width], src_ap.rearrange("(k p) n -> p k n", p=128), [src_res], [wst[i]])
        P.copy("pool", wbf[i][:, 0:4, 0:width], wst[i][:, 0:4, 0:width], [wst[i]], [wbf[i]])
        P.copy("dve", wbf[i][:, 4:8, 0:width], wst[i][:, 4:8, 0:width], [wst[i]], [wbf[i]])
        return wbf[i]

    cmn = {"n": 0}

    def proj_cm(w, width, dst_ap, dst_res, bf):
        for j in range(0, width, 128):
            m = min(128, width - j)
            i = cmn["n"] % 2
            cmn["n"] += 1
            stg = cmsb[i] if bf else cmst[i]
            for (t0, tn) in TB:
                ps = next_ps()
                for k in range(8):
                    P.mm(ps[0:m, 0:tn], w[:, k, j:j + m], hT[:, k, t0:t0 + tn], k == 0, k == 7, [w, hT], [ps])
                evac(stg[0:m, t0:t0 + tn], ps[0:m, 0:tn], [ps], [stg])
            P.dma("pool", dst_ap[j:j + m, :], stg[0:m, :], [stg], [dst_res])

    tmn = {"n": 0}

    def proj_tm(w, dst_ap, dst_res, bf):
        for t in range(18):
            i = tmn["n"] % 2
            tmn["n"] += 1
            stg = tmsb[i] if bf else tmst[i]
            ps = next_ps()
            for k in range(8):
                P.mm(ps[:, :], hT[:, k, t * 128:(t + 1) * 128], w[:, k, :], k == 0, k == 7, [w, hT], [ps])
            evac(stg[:], ps[:], [ps], [stg])
            P.dma("pool", dst_ap[t * 128:(t + 1) * 128, :], stg[:], [stg], [dst_res])

    def wcols(c0, width):
        return win_d[:, c0:c0 + width]

    w = load_w(wcols(0, 512), win_d, 512);    proj_cm(w, 512, uaT_o[:], uaT_o, False)
    w = load_w(wcols(512, 512), win_d, 512);  proj_cm(w, 512, zT_o[0:512, :], zT_o, False)
    w = load_w(wcols(1024, 512), win_d, 512); proj_cm(w, 512, naq_o[:], naq_o, True)
    w = load_w(wcols(1536, 512), win_d, 512); proj_cm(w, 512, nak_o[:], nak_o, True)
    w = load_w(wcols(2048, 512), win_d, 512); proj_tm(w, nav_o[:], nav_o, True)
    w = load_w(wcols(2560, 512), win_d, 512); proj_cm(w, 512, zT_o[512:1024, :], zT_o, False)
    w = load_w(wcols(3488, 512), win_d, 512); proj_cm(w, 512, zT_o[1024:1536, :], zT_o, False)
    for g in range(6):
        w = load_w(wcols(4000 + g * 512, 512), win_d, 512)
        proj_cm(w, 512, g3_o[g * 512:(g + 1) * 512, :], g3_o, False)

    wm = load_w(wcols(3072, 416), win_d, 416)
    wk = load_w(wkrs_d[:, :], wkrs_d, 32)
    uqst = cmst[0]; uq = P.sbuf("uq", [128, 2, 768], BF16)
    uq2st = cmst[1]; uq2 = P.sbuf("uq2", [128, 2, 768], BF16)
    ukvst = xt[0]; ukvk = P.sbuf("ukvk", [128, 8, 64], BF16); ukvv = P.sbuf("ukvv", [128, 8, 64], BF16)
    qn = P.sbuf("qn", [128, 2]); kvn = P.sbuf("kvn", [128, 1])
    P.dma("sp", uqst[:, 0:1536].rearrange("p (k n) -> p k n", k=2), wuq_d[:].rearrange("(k p) n -> p k n", p=128), [wuq_d], [uqst])
    P.dma("sp", uq2st[:, 0:1536].rearrange("p (k n) -> p k n", k=2), wuq2_d[:].rearrange("(k p) n -> p k n", p=128), [wuq2_d], [uq2st])
    P.dma("sp", ukvst[:], wukv_d[:], [wukv_d], [ukvst])
    P.dma("sp", qn[:], qn_d[:], [qn_d], [qn])
    P.dma("sp", kvn[:], kvn_d[:], [kvn_d], [kvn])
    for k in range(2):
        P.ts("dve", uq[:, k, :], uqst[:, k * 768:(k + 1) * 768], qn[:, k:k + 1], None, ALU.mult, None, [uqst, qn], [uq])
        P.ts("dve", uq2[:, k, :], uq2st[:, k * 768:(k + 1) * 768], qn[:, k:k + 1], None, ALU.mult, None, [uq2st, qn], [uq2])
    u3 = ukvst[:].rearrange("p (h c) -> p h c", c=128)
    P.ts("dve", ukvk[:], u3[:, :, 0:64], kvn[:, 0:1], None, ALU.mult, None, [ukvst, kvn], [ukvk])
    P.ts("dve", ukvv[:], u3[:, :, 64:128], kvn[:, 0:1], None, ALU.mult, None, [ukvst, kvn], [ukvv])

    cq = [P.sbuf(f"cq{i}", [128, 2, 512]) for i in range(2)]
    ckv = [P.sbuf(f"ckv{i}", [128, 512]) for i in range(2)]
    kr = [P.sbuf(f"kr{i}", [32, 2, 512]) for i in range(2)]
    sq = [P.sbuf("sq", [128, 3, 512], BF16)] * 2
    rs = [P.sbuf("rs", [128, 2, 512])] * 2
    cqn = [P.sbuf(f"cqn{i}", [128, 2, 512], BF16) for i in range(2)]
    ckvn = [P.sbuf(f"ckvn{i}", [128, 512], BF16) for i in range(2)]
    tb96 = [P.sbuf("tb96", [96, 2, 512])] * 2
    tb32 = [P.sbuf("tb32", [32, 2, 512])] * 2
    t1 = [P.sbuf(f"t1{i}", [96, 512]) for i in range(2)]
    t2 = [P.sbuf(f"t2{i}", [96, 512]) for i in range(2)]
    qo = [P.sbuf(f"qo{i}", [96, 512], BF16) for i in range(2)]
    kno = [P.sbuf(f"kno{i}", [128, 512], BF16) for i in range(2)]
    kpo = [P.sbuf(f"kpo{i}", [32, 512], BF16) for i in range(2)]
    vo = [P.sbuf(f"vo{i}", [128, 512], BF16) for i in range(2)]
    nq = {"n": 0}
    for bi, (t0, tn) in enumerate(TB):
        b2 = bi % 2
        CQ, CKV, KR, SQ, RS, CQN, CKVN, T96, T32 = cq[b2], ckv[b2], kr[b2], sq[b2], rs[b2], cqn[b2], ckvn[b2], tb96[b2], tb32[b2]
        P.dma("sp", T96[:, 0, 0:tn], c96_d[:, t0:t0 + tn], [c96_d], [T96])
        P.dma("sp", T96[:, 1, 0:tn], s96_d[:, t0:t0 + tn], [s96_d], [T96])
        P.dma("sp", T32[:, 0, 0:tn], c32_d[:, t0:t0 + tn], [c32_d], [T32])
        P.dma("sp", T32[:, 1, 0:tn], s32_d[:, t0:t0 + tn], [s32_d], [T32])
        for (c0, m, dst) in ((0, 128, CQ[:, 0, 0:tn]), (128, 128, CQ[:, 1, 0:tn]), (256, 128, CKV[:, 0:tn]), (384, 32, KR[:, 0, 0:tn])):
            ps = next_ps()
            for k in range(8):
                P.mm(ps[0:m, 0:tn], wm[:, k, c0:c0 + m], hT[:, k, t0:t0 + tn], k == 0, k == 7, [wm, hT], [ps])
            evac(dst, ps[0:m, 0:tn], [ps], [CQ, CKV, KR])
        ps = next_ps()
        for k in range(8):
            P.mm(ps[0:32, 0:tn], wk[:, k, 0:32], hT[:, k, t0:t0 + tn], k == 0, k == 7, [wk, hT], [ps])
        evac(KR[:, 1, 0:tn], ps[0:32, 0:tn], [ps], [KR])
        P.act(SQ[:, 0:2, 0:tn], CQ[:, :, 0:tn], AF.Square, [CQ], [SQ])
        P.act(SQ[:, 2, 0:tn], CKV[:, 0:tn], AF.Square, [CKV], [SQ])
        P.mm(psB[:, 0:tn], ones[:], SQ[:, 0, 0:tn], True, False, [ones, SQ], [psB])
        P.mm(psB[:, 0:tn], ones[:], SQ[:, 1, 0:tn], False, True, [ones, SQ], [psB])
        P.act(RS[:, 0, 0:tn], psB[:, 0:tn], AF.Sqrt, [psB], [RS], bias=RMS_EPS, scale=1.0 / 256)
        P.mm(psB[:, 0:tn], ones[:], SQ[:, 2, 0:tn], True, True, [ones, SQ, RS], [psB])
        P.act(RS[:, 1, 0:tn], psB[:, 0:tn], AF.Sqrt, [psB], [RS], bias=RMS_EPS, scale=1.0 / 128)
        P.op("dve", lambda RS=RS, tn=tn: nc.vector.reciprocal(RS[:, :, 0:tn], RS[:, :, 0:tn]), [RS], [RS])
        P.tt("dve", CQN[:, :, 0:tn], CQ[:, :, 0:tn], RS[:, 0:1, 0:tn].to_broadcast([128, 2, tn]), ALU.mult, [CQ, RS], [CQN])
        P.tt("pool", CKVN[:, 0:tn], CKV[:, 0:tn], RS[:, 1, 0:tn], ALU.mult, [CKV, RS], [CKVN])
        for h in range(8):
            i = nq["n"] % 2
            nq["n"] += 1
            pa, pb = next_ps(), next_ps()
            for k in range(2):
                P.mm(pa[0:96, 0:tn], uq[:, k, h * 96:(h + 1) * 96], CQN[:, k, 0:tn], k == 0, k == 1, [uq, CQN], [pa])
            for k in range(2):
                P.mm(pb[0:96, 0:tn], uq2[:, k, h * 96:(h + 1) * 96], CQN[:, k, 0:tn], k == 0, k == 1, [uq2, CQN], [pb])
            P.tt("dve", t1[i][:, 0:tn], pa[0:96, 0:tn], T96[:, 0, 0:tn], ALU.mult, [pa, T96], [t1[i]])
            P.tt("dve", t2[i][:, 0:tn], pb[0:96, 0:tn], T96[:, 1, 0:tn], ALU.mult, [pb, T96], [t2[i]])
            P.tt("pool", qo[i][:, 0:tn], t1[i][:, 0:tn], t2[i][:, 0:tn], ALU.add, [t1[i], t2[i]], [qo[i]])
            P.dma("pool", mq_o[h, :, t0:t0 + tn], qo[i][:, 0:tn], [qo[i]], [mq_o])
        for hp in range(4):
            i = nq["n"] % 2
            nq["n"] += 1
            ps = next_ps()
            P.mm(ps[:, 0:tn], ukvk[:].rearrange("p h c -> p (h c)")[:, hp * 128:(hp + 1) * 128], CKVN[:, 0:tn], True, True, [ukvk, CKVN], [ps])
            evac(kno[i][:, 0:tn], ps[:, 0:tn], [ps], [kno[i]])
            P.dma("pool", mkn_o[hp * 128:(hp + 1) * 128, t0:t0 + tn], kno[i][:, 0:tn], [kno[i]], [mkn_o])
        for tt_ in range(tn // 128):
            i = nq["n"] % 2
            nq["n"] += 1
            ps = next_ps()
            P.mm(ps[:, :], CKVN[:, tt_ * 128:(tt_ + 1) * 128], ukvv[:].rearrange("p h c -> p (h c)"), True, True, [ukvv, CKVN], [ps])
            evac(vo[i][:], ps[:], [ps], [vo[i]])
            P.dma("pool", mv_o[t0 + tt_ * 128:t0 + (tt_ + 1) * 128, :], vo[i][:], [vo[i]], [mv_o])
        i = bi % 2
        P.tt("dve", t1[i][0:32, 0:tn], KR[:, 0, 0:tn], T32[:, 0, 0:tn], ALU.mult, [KR, T32], [t1[i]])
        P.tt("dve", t2[i][0:32, 0:tn], KR[:, 1, 0:tn], T32[:, 1, 0:tn], ALU.mult, [KR, T32], [t2[i]])
        P.tt("pool", kpo[i][:, 0:tn], t1[i][0:32, 0:tn], t2[i][0:32, 0:tn], ALU.add, [t1[i], t2[i]], [kpo[i]])
        P.dma("pool", mkpe_o[:, t0:t0 + tn], kpo[i][:, 0:tn], [kpo[i]], [mkpe_o])

    P.fence("pool", outs)
    P.emit()
    return P


import math

NCH = 1056
CB = [(0, 512), (512, 512), (1024, 32)]
PI = math.pi
CW1 = 6.28125
CW2 = 2 * math.pi - 6.28125


def build_s5():
    P = Prog()
    nc = P.nc
    u_d = P.dram_in("u", [4, 2, 2, 128, NCH])
    par_d = P.dram_in("par", [128, 12])
    bc_d = P.dram_in("bc", [128, 4, 4, 16])
    mask_d = P.dram_in("tmask", [128, 128])
    idd = P.dram_in("ident", [128, 128])
    y_o = P.dram_out("y", [4, 2, 2, 128, NCH])

    ident = P.sbuf("ident", [128, 128])
    tmask = P.sbuf("tmask", [128, 128])
    par = P.sbuf("par", [128, 12])
    bcp = P.sbuf("bcp", [128, 4, 4, 16])
    P.dma("sp", ident[:], idd[:], [idd], [ident])
    P.dma("sp", tmask[:], mask_d[:], [mask_d], [tmask])
    P.dma("sp", par[:], par_d[:], [par_d], [par])
    P.dma("sp", bcp[:], bc_d[:], [bc_d], [bcp])

    sm = P.sbuf("sm", [128, 32, 4])
    SM = [sm]

    def col(i):
        return sm[:, i, :]
    (DT, ZR, ANG, MAG, KK, R, R2, WW, SIN, COS, ARE, AIM, NR, DEN, RDEN, FR, FI, T1, T2, T3, M2, RM, I8R, I8I) = range(24)
    lr, li, ldt = par[:, 0:4], par[:, 4:8], par[:, 8:12]
    D = "dve"
    P.act(col(DT), ldt, AF.Exp, [par], SM)
    P.tt(D, col(ZR), lr, col(DT), ALU.mult, [par] + SM, SM)
    P.tt(D, col(ANG), li, col(DT), ALU.mult, [par] + SM, SM)
    P.act(col(MAG), col(ZR), AF.Exp, SM, SM)
    P.ts(D, col(KK), col(ANG), PI, None, ALU.is_gt, None, SM, SM)
    for m in (1, 2, 3):
        P.stt(D, col(KK), col(ANG), (2 * m + 1) * PI, col(KK), ALU.is_gt, ALU.add, SM, SM)
    P.stt(D, col(R), col(KK), -CW1, col(ANG), ALU.mult, ALU.add, SM, SM)
    P.stt(D, col(R), col(KK), -CW2, col(R), ALU.mult, ALU.add, SM, SM)
    P.act(col(SIN), col(R), AF.Sin, SM, SM)
    P.ts(D, col(R2), col(R), PI / 2, None, ALU.add, None, SM, SM)
    P.ts(D, col(WW), col(R2), PI, None, ALU.is_gt, None, SM, SM)
    P.stt(D, col(R2), col(WW), -2 * PI, col(R2), ALU.mult, ALU.add, SM, SM)
    P.act(col(COS), col(R2), AF.Sin, SM, SM)
    P.tt(D, col(ARE), col(MAG), col(COS), ALU.mult, SM, SM)
    P.tt(D, col(AIM), col(MAG), col(SIN), ALU.mult, SM, SM)
    P.ts(D, col(NR), col(ARE), -1.0, None, ALU.add, None, SM, SM)
    P.tt(D, col(T1), lr, lr, ALU.mult, [par] + SM, SM)
    P.tt(D, col(T2), li, li, ALU.mult, [par] + SM, SM)
    P.tt(D, col(DEN), col(T1), col(T2), ALU.add, SM, SM)
    P.op(D, lambda: nc.vector.reciprocal(col(RDEN), col(DEN)), SM, SM)
    P.tt(D, col(T1), col(NR), lr, ALU.mult, [par] + SM, SM)
    P.tt(D, col(T2), col(AIM), li, ALU.mult, [par] + SM, SM)
    P.tt(D, col(T1), col(T1), col(T2), ALU.add, SM, SM)
    P.tt(D, col(FR), col(T1), col(RDEN), ALU.mult, SM, SM)
    P.tt(D, col(T1), col(AIM), lr, ALU.mult, [par] + SM, SM)
    P.tt(D, col(T2), col(NR), li, ALU.mult, [par] + SM, SM)
    P.tt(D, col(T1), col(T1), col(T2), ALU.subtract, SM, SM)
    P.tt(D, col(FI), col(T1), col(RDEN), ALU.mult, SM, SM)

    big = [P.sbuf(f"big{i}", [128, 4, 128]) for i in range(2)]

    def cmul(o_re, o_im, a_re, a_im, b_re, b_im, shape, reads, writes, neg_im=False):
        n = 1
        for s_ in shape[1:]:
            n *= s_
        def tv(t):
            v = t[:].rearrange("p a b -> p (a b)")[:, 0:n]
            if len(shape) == 3:
                return v.rearrange("p (a b) -> p a b", a=shape[1])
            if len(shape) == 4:
                return v.rearrange("p (a b c) -> p a b c", a=shape[1], b=shape[2])
            return v
        t1, t2 = tv(big[0]), tv(big[1])
        P.tt(D, t1, a_re, b_re, ALU.mult, reads, [big[0]])
        P.tt(D, t2, a_im, b_im, ALU.mult, reads, [big[1]])
        P.tt(D, o_re, t1, t2, ALU.subtract, big, writes)
        P.tt(D, t1, a_re, b_im, ALU.mult, reads + writes, [big[0]])
        P.tt(D, t2, a_im, b_re, ALU.mult, reads + writes, [big[1]])
        if neg_im:
            P.stt(D, o_im, t1, -1.0, t2, ALU.mult, ALU.subtract, big, writes)
        else:
            P.tt(D, o_im, t1, t2, ALU.add, big, writes)

    bb = P.sbuf("bb", [128, 2, 4, 16])
    fr_b = col(FR).unsqueeze(2).to_broadcast([128, 4, 16])
    fi_b = col(FI).unsqueeze(2).to_broadcast([128, 4, 16])
    cmul(bb[:, 0], bb[:, 1], fr_b, fi_b, bcp[:, :, 0, :], bcp[:, :, 1, :], [128, 4, 16], SM + [bcp], [bb])

    pw = P.sbuf("pw", [128, 2, 4, 9])
    P.memset("pool", pw[:, 0, :, 0:1], 1.0, [pw])
    P.memset("pool", pw[:, 1, :, 0:1], 0.0, [pw])
    P.copy(D, pw[:, 0, :, 1], col(ARE), SM, [pw])
    P.copy(D, pw[:, 1, :, 1], col(AIM), SM, [pw])
    for (lo, n, by) in ((2, 1, 1), (3, 2, 2), (5, 4, 4)):
        cmul(pw[:, 0, :, lo:lo + n], pw[:, 1, :, lo:lo + n], pw[:, 0, :, 1:1 + n], pw[:, 1, :, 1:1 + n],
             pw[:, 0, :, by:by + 1].to_broadcast([128, 4, n]), pw[:, 1, :, by:by + 1].to_broadcast([128, 4, n]),
             [128, 4, n], [pw], [pw])

    Wc = P.sbuf("Wc", [128, 2, 4, 128])
    R1 = P.sbuf("R1", [128, 2, 4, 128])
    Wq = P.sbuf("Wq", [128, 2, 4, 128])

    def v4(t, c):
        return t[:, c].rearrange("p q (s j) -> p q s j", j=16)
    cmul(v4(Wc, 0), v4(Wc, 1),
         pw[:, 0, :, 0:8].unsqueeze(3).to_broadcast([128, 4, 8, 16]), pw[:, 1, :, 0:8].unsqueeze(3).to_broadcast([128, 4, 8, 16]),
         bb[:, 0].unsqueeze(2).to_broadcast([128, 4, 8, 16]), bb[:, 1].unsqueeze(2).to_broadcast([128, 4, 8, 16]),
         [128, 4, 8, 16], [pw, bb], [Wc])
    cmul(v4(R1, 0), v4(R1, 1),
         pw[:, 0, :, 1:9].unsqueeze(3).to_broadcast([128, 4, 8, 16]), pw[:, 1, :, 1:9].unsqueeze(3).to_broadcast([128, 4, 8, 16]),
         bcp[:, :, 2, :].unsqueeze(2).to_broadcast([128, 4, 8, 16]), bcp[:, :, 3, :].unsqueeze(2).to_broadcast([128, 4, 8, 16]),
         [128, 4, 8, 16], [pw, bcp], [R1], neg_im=True)
    p8r, p8i = pw[:, 0, :, 8], pw[:, 1, :, 8]
    P.tt(D, col(T1), p8r, p8r, ALU.mult, [pw], SM)
    P.tt(D, col(T2), p8i, p8i, ALU.mult, [pw], SM)
    P.tt(D, col(M2), col(T1), col(T2), ALU.add, SM, SM)
    P.op(D, lambda: nc.vector.reciprocal(col(RM), col(M2)), SM, SM)
    P.tt(D, col(I8R), p8r, col(RM), ALU.mult, [pw] + SM, SM)
    P.stt(D, col(I8I), p8i, -1.0, col(RM), ALU.mult, ALU.mult, [pw] + SM, SM)
    cmul(Wq[:, 0], Wq[:, 1], col(I8R).unsqueeze(2).to_broadcast([128, 4, 128]), col(I8I).unsqueeze(2).to_broadcast([128, 4, 128]),
         Wc[:, 0], Wc[:, 1], [128, 4, 128], SM + [Wc], [Wq])

    NST = 11
    AA = P.sbuf("AA", [128, 3, 4, NST])
    P.copy(D, AA[:, 0, :, 0], p8r, [pw], [AA])
    P.copy(D, AA[:, 1, :, 0], p8i, [pw], [AA])
    for j in range(1, NST):
        P.tt(D, col(T1), AA[:, 0, :, j - 1], AA[:, 0, :, j - 1], ALU.mult, [AA], SM)
        P.tt(D, col(T2), AA[:, 1, :, j - 1], AA[:, 1, :, j - 1], ALU.mult, [AA], SM)
        P.tt(D, AA[:, 0, :, j], col(T1), col(T2), ALU.subtract, SM, [AA])
        P.stt(D, AA[:, 1, :, j], AA[:, 0, :, j - 1], 2.0, AA[:, 1, :, j - 1], ALU.mult, ALU.mult, [AA], [AA])
    P.ts(D, AA[:, 2], AA[:, 1], -1.0, None, ALU.mult, None, [AA], [AA])

    WtT = P.sbuf("WtT", [128, 4, 2, 128], BF16)
    Tt = P.sbuf("Tt", [128, 4, 2, 128], BF16)
    R1b = P.sbuf("R1b", [128, 2, 4, 128], BF16)
    ps = [P.psum(f"ps{i}", [128, 512]) for i in range(6)]
    rr = {"n": 0}

    def nps():
        rr["n"] += 1
        return ps[rr["n"] % 6]
    P.copy("pool", R1b[:], R1[:], [R1], [R1b])
    for q in range(4):
        for c in range(2):
            pt = nps()
            P.tr(pt[:, 0:128], Wc[:, c, q, :], ident[:], [Wc, ident], [pt])
            P.copy("act", WtT[:, q, c, :], pt[:, 0:128], [pt], [WtT])
        for d in range(2):
            pt = nps()
            sl = slice(d * 64, d * 64 + 64)
            P.mm(pt[:, 0:128], Wq[sl, 0, q, :], R1[sl, 0, q, :], True, False, [Wq, R1], [pt])
            P.mm(pt[:, 0:128], Wq[sl, 1, q, :], R1[sl, 1, q, :], False, True, [Wq, R1], [pt])
            P.tt(D, Tt[:, q, d, :], pt[:, 0:128], tmask[:], ALU.mult, [pt, tmask], [Tt])

    ust = [P.sbuf(f"ust{i}", [128, NCH]) for i in range(2)]
    ub = [[[P.sbuf(f"ub{i}{d}{b}", [128, NCH], BF16) for b in range(2)] for d in range(2)] for i in range(2)]
    S = [[P.sbuf(f"S{i}{c}", [128, 2, NCH + 1]) for c in range(2)] for i in range(2)]
    Sb = [P.sbuf(f"Sb{c}", [128, 2, NCH + 1], BF16) for c in range(2)]
    yst = [P.sbuf(f"yst{i}", [128, NCH]) for i in range(2)]
    nu = {"n": 0}
    for q in range(4):
        U = ub[q % 2]
        for d in range(2):
            for b in range(2):
                i = nu["n"] % 2
                nu["n"] += 1
                P.dma("sp", ust[i][:], u_d[q, d, b], [u_d], [ust[i]])
                P.copy("pool", U[d][b][:], ust[i][:], [ust[i]], [U[d][b]])
        cur = S[0]
        for c in range(2):
            P.memset("pool", cur[c][:, :, 0:1], 0.0, [cur[c]])
        for b in range(2):
            for (k0, kn) in CB:
                for c in range(2):
                    pt = nps()
                    for d in range(2):
                        P.mm(pt[d * 64:d * 64 + 64, 0:kn], WtT[:, q, c, d * 64:d * 64 + 64], U[d][b][:, k0:k0 + kn], True, True,
                             [WtT, U[d][b]], [pt])
                    P.copy("act", cur[c][:, b, 1 + k0:1 + k0 + kn], pt[:, 0:kn], [pt], [cur[c]])
        L = NCH + 1
        for j in range(NST):
            sh = 1 << j
            nxt = S[(j + 1) % 2]
            ar, ai, nai = AA[:, 0, q, j:j + 1], AA[:, 1, q, j:j + 1], AA[:, 2, q, j:j + 1]
            for c in range(2):
                P.copy("pool", nxt[c][:, :, 0:sh], cur[c][:, :, 0:sh], [cur[c]], [nxt[c]])
            P.stt(D, nxt[0][:, :, sh:L], cur[0][:, :, 0:L - sh], ar, cur[0][:, :, sh:L], ALU.mult, ALU.add, [cur[0], AA], [nxt[0]])
            P.stt(D, nxt[0][:, :, sh:L], cur[1][:, :, 0:L - sh], nai, nxt[0][:, :, sh:L], ALU.mult, ALU.add, [cur[1], AA], [nxt[0]])
            P.stt(D, nxt[1][:, :, sh:L], cur[1][:, :, 0:L - sh], ar, cur[1][:, :, sh:L], ALU.mult, ALU.add, [cur[1], AA], [nxt[1]])
            P.stt(D, nxt[1][:, :, sh:L], cur[0][:, :, 0:L - sh], ai, nxt[1][:, :, sh:L], ALU.mult, ALU.add, [cur[0], AA], [nxt[1]])
            cur = nxt
        for c in range(2):
            P.copy("act" if c else "pool", Sb[c][:], cur[c][:], [cur[c]], [Sb[c]])
        for d in range(2):
            sl = slice(d * 64, d * 64 + 64)
            for b in range(2):
                i = nu["n"] % 2
                nu["n"] += 1
                for (k0, kn) in CB:
                    pt = nps()
                    P.mm(pt[:, 0:kn], Tt[:, q, d, :], U[d][b][:, k0:k0 + kn], True, False, [Tt, U[d][b]], [pt])
                    P.mm(pt[:, 0:kn], R1b[sl, 0, q, :], Sb[0][sl, b, k0:k0 + kn], False, False, [R1b, Sb[0]], [pt])
                    P.mm(pt[:, 0:kn], R1b[sl, 1, q, :], Sb[1][sl, b, k0:k0 + kn], False, True, [R1b, Sb[1]], [pt])
                    P.copy("act", yst[i][:, k0:k0 + kn], pt[:, 0:kn], [pt], [yst[i]])
                P.dma("pool", y_o[q, d, b], yst[i][:], [yst[i]], [y_o])
    P.fence("pool", [y_o])
    P.emit()
    return P


NTOK = 2304
NKEY = 8448
NKT = 66
MLA_SCALE = 96 ** -0.5
NA_SCALE = 0.125
QB = [(0, 512), (512, 512), (1024, 512), (1536, 512)]


def build_attn():
    P = Prog()
    nc = P.nc
    mq_d = P.dram_in("mq", [8, 96, NTOK], BF16)
    mk_d = P.dram_in("mk", [8, 96, NKEY], BF16)
    mv_d = P.dram_in("mv", [8, 128, NKT * 65], BF16)
    nq_d = P.dram_in("nq", [8, 64, NTOK], BF16)
    nk_d = P.dram_in("nk", [8, 64, 2816], BF16)
    nv_d = P.dram_in("nv", [8, 128, 22 * 65], BF16)
    gp_d = P.dram_in("gp", [8, 128, 1408])
    am_d = P.dram_in("am", [8, 32 * 128], BF16)
    bq_d = P.dram_in("bq", [8, 512], BF16)
    ym_o = P.dram_out("ymT", [512, NTOK])
    yn_o = P.dram_out("ynT", [512, NTOK])

    onesf = P.sbuf("onesf", [128, 64])
    am = P.sbuf("am", [8, 32 * 128], BF16)
    bq = P.sbuf("bq", [8, 512], BF16)
    P.memset("pool", onesf[:], 1.0, [onesf])
    P.dma("sp", am[:], am_d[:], [am_d], [am])
    P.dma("sp", bq[:], bq_d[:], [bq_d], [bq])

    mk = [P.sbuf(f"mk{i}", [96, NKEY], BF16) for i in range(2)]
    mv = [P.sbuf(f"mv{i}", [128, NKT * 65], BF16) for i in range(2)]
    mq = [P.sbuf(f"mq{i}", [96, NTOK], BF16) for i in range(2)]
    nk = [P.sbuf(f"nk{i}", [64, 2816], BF16) for i in range(2)]
    nv = [P.sbuf(f"nv{i}", [128, 22 * 65], BF16) for i in range(2)]
    nq = [P.sbuf(f"nq{i}", [64, NTOK], BF16) for i in range(2)]
    E = [P.sbuf(f"E{i}", [128, 1408]) for i in range(2)]
    pT = [P.sbuf(f"pT{i}", [128, 512], BF16) for i in range(4)]
    tmp = [P.sbuf(f"tmp{i}", [128, 512]) for i in range(2)]
    rc = P.sbuf("rc", [128, 512])
    osb = [P.sbuf(f"osb{i}", [64, 512]) for i in range(2)]
    ost = [P.sbuf(f"ost{i}", [64, 512]) for i in range(2)]
    psS = [P.psum(f"psS{i}", [128, 512]) for i in range(4)]
    psO = [P.psum(f"psO{i}", [128, 512]) for i in range(2)]
    psB = [P.psum(f"psB{i}", [128, 512]) for i in range(2)]
    rr = {"s": 0, "p": 0, "o": 0, "t": 0, "f": 0}

    def nxt(lst, key):
        rr[key] += 1
        return lst[rr[key] % len(lst)]

    def finish(O, n, dst_ap, dst_res):
        B_ = nxt(psB, "f")
        i = rr["f"] % 2
        P.op("dve", lambda: nc.vector.reciprocal(rc[64:65, 0:n], O[64:65, 0:n]), [O], [rc])
        P.mm(B_[0:64, 0:n], onesf[64:65, 0:64], rc[64:65, 0:n], True, True, [onesf, rc], [B_])
        P.copy("act", osb[i][:, 0:n], O[0:64, 0:n], [O], [osb[i]])
        P.tt("dve", ost[i][:, 0:n], osb[i][:, 0:n], B_[0:64, 0:n], ALU.mult, [osb[i], B_], [ost[i]])
        P.dma("pool", dst_ap, ost[i][:, 0:n], [ost[i]], [dst_res])

    for h in range(8):
        b2 = h % 2
        MK, MV, MQ, NK, NV, NQ, EE = mk[b2], mv[b2], mq[b2], nk[b2], nv[b2], nq[b2], E[b2]
        P.dma("sp", MQ[:], mq_d[h], [mq_d], [MQ])
        P.dma("sp", MK[:], mk_d[h], [mk_d], [MK])
        P.dma("sp", MV[:], mv_d[h], [mv_d], [MV])
        P.dma("sp", NQ[:], nq_d[h], [nq_d], [NQ])
        P.dma("sp", NK[:], nk_d[h], [nk_d], [NK])
        P.dma("sp", NV[:], nv_d[h], [nv_d], [NV])
        P.dma("sp", EE[:], gp_d[h], [gp_d], [EE])
        P.act(EE[:], EE[:], AF.Exp, [EE], [EE])

        for (q0, qn_, tiles) in [(q0, qn_, range(NKT)) for (q0, qn_) in QB] + [(2048, 256, (64, 65))]:
            O = nxt(psO, "o")
            tl = list(tiles)
            for ti, kt in enumerate(tl):
                S = nxt(psS, "s")
                pt = nxt(pT, "p")
                P.mm(S[:, 0:qn_], MK[:, kt * 128:(kt + 1) * 128], MQ[:, q0:q0 + qn_], True, True, [MK, MQ], [S])
                P.act(pt[:, 0:qn_], S[:, 0:qn_], AF.Exp, [S], [pt], scale=MLA_SCALE)
                P.mm(O[0:65, 0:qn_], MV[:, kt * 65:(kt + 1) * 65], pt[:, 0:qn_], ti == 0, ti == len(tl) - 1, [MV, pt], [O])
            finish(O, qn_, ym_o[h * 64:(h + 1) * 64, q0:q0 + qn_], ym_o)

        for blk, (q0, qn_) in enumerate(QB):
            O = nxt(psO, "o")
            for i in range(8):
                kt = 4 * blk + i
                S = nxt(psS, "s")
                pt = nxt(pT, "p")
                tp = nxt(tmp, "t")
                P.mm(S[:, 0:512], NK[:, kt * 128:(kt + 1) * 128], NQ[:, q0:q0 + 512], True, False, [NK, NQ], [S])
                P.mm(S[:, 0:512], am[:, (blk * 8 + i) * 128:(blk * 8 + i + 1) * 128], bq[:], False, True, [am, bq], [S])
                P.act(tp[:], S[:], AF.Exp, [S], [tp], scale=NA_SCALE)
                P.tt("dve", pt[:], tp[:], EE[:, (14 - 2 * i) * 64:(14 - 2 * i) * 64 + 512], ALU.mult, [tp, EE], [pt])
                P.mm(O[0:65, 0:512], NV[:, kt * 65:(kt + 1) * 65], pt[:], i == 0, False, [NV, pt], [O])
            for c in range(2):
                S = nxt(psS, "s")
                pt = nxt(pT, "p")
                P.mm(S[:, 0:512], NK[:, 2560 + c * 128:2560 + (c + 1) * 128], NQ[:, q0:q0 + 512], True, True, [NK, NQ], [S])
                P.act(pt[:], S[:], AF.Exp, [S], [pt], scale=NA_SCALE)
                P.mm(O[0:65, 0:512], NV[:, (20 + c) * 65:(21 + c) * 65], pt[:], False, c == 1, [NV, pt], [O])
            finish(O, 512, yn_o[h * 64:(h + 1) * 64, q0:q0 + 512], yn_o)
        O = nxt(psO, "o")
        for c in range(2):
            S = nxt(psS, "s")
            pt = nxt(pT, "p")
            P.mm(S[:, 0:256], NK[:, 2560 + c * 128:2560 + (c + 1) * 128], NQ[:, 2048:2304], True, True, [NK, NQ], [S])
            P.act(pt[:, 0:256], S[:, 0:256], AF.Exp, [S], [pt], scale=NA_SCALE)
            P.mm(O[0:65, 0:256], NV[:, (20 + c) * 65:(21 + c) * 65], pt[:, 0:256], c == 0, c == 1, [NV, pt], [O])
        finish(O, 256, yn_o[h * 64:(h + 1) * 64, 2048:2304], yn_o)

    P.fence("pool", [ym_o, yn_o])
    P.emit()
    return P


NTOK = 2304
TB = [(0, 512), (512, 512), (1024, 512), (1536, 512), (2048, 256)]
ALPHA = 8 ** 0.25
LN_EPS = 1e-5


def build_l3():
    P = Prog()
    nc = P.nc
    x_d = P.dram_in("x", [NTOK, 1024])
    yf_d = P.dram_in("yfT", [512, NTOK])
    yb_d = P.dram_in("ybT", [512, NTOK])
    ua_d = P.dram_in("uaT", [512, NTOK])
    z_d = P.dram_in("zT", [1536, NTOK])
    yn_d = P.dram_in("ynT", [512, NTOK])
    ym_d = P.dram_in("ymT", [512, NTOK])
    g_d = P.dram_in("gT", [3072, NTOK])
    csc_d = P.dram_in("csc", [128, 16])
    dsk_d = P.dram_in("dskip", [128, 4])
    bglu_d = P.dram_in("bglu", [128, 4])
    wglu_d = P.dram_in("w_glu", [512, 512])
    wa_d = P.dram_in("w_a", [512, 1024])
    wb_d = P.dram_in("w_b", [512, 1024])
    wc_d = P.dram_in("w_c", [512, 1024])
    wo_d = P.dram_in("w_o", [1024, 1024])
    wg_d = P.dram_in("w_gate", [1024, 1024])
    bg_d = P.dram_in("b_gate", [128, 1024])
    lng_d = P.dram_in("ln_g", [128, 1024])
    lnb_d = P.dram_in("ln_b", [128, 1024])
    xo = P.dram_out("xo", [NTOK, 1024])

    wst = P.sbuf("wst", [128, 8, 512])
    wglu = P.sbuf("wglu", [128, 4, 512], BF16)
    wbr = [P.sbuf(f"wbr{i}", [128, 4, 1024], BF16) for i in range(3)]
    wout = P.sbuf("wout", [128, 8, 1024], BF16)
    csc = P.sbuf("csc", [128, 16]); sc = P.sbuf("sc", [128, 16])
    dsk = P.sbuf("dsk", [128, 4]); bglu = P.sbuf("bglu", [128, 4])
    gate = P.sbuf("gate", [128, 2, 1024])
    lng = P.sbuf("lng", [128, 1024]); lnb = P.sbuf("lnb", [128, 1024])
    tbr = [P.sbuf(f"tbr{i}", [128, 4, NTOK], BF16) for i in range(3)]
    ps = [P.psum(f"ps{i}", [128, 512]) for i in range(8)]
    rr = {"p": 0, "l": 0}

    def nps():
        rr["p"] += 1
        return ps[rr["p"] % 8]

    for (t, d) in ((csc, csc_d), (dsk, dsk_d), (bglu, bglu_d), (lng, lng_d), (lnb, lnb_d)):
        P.dma("sp", t[:], d[:], [d], [t])
    P.dma("sp", gate[:, 0, :], bg_d[:], [bg_d], [gate])
    P.dma("sp", gate[:, 1, :], bg_d[:], [bg_d], [gate])
    P.act(sc[:], csc[:], AF.Silu, [csc], [sc])
    gg = P.sbuf("gg", [128, 4, 512]); scb = Tile(gg.h[:].rearrange("p a b -> p (a b)").rearrange("p (c k m) -> p c k m", c=2, k=8), "scb"); scb.res = gg.res
    sc3 = sc[:].rearrange("p (k c) -> p k c", c=2)
    for c in range(2):
        P.copy("dve", scb[:, c], sc3[:, :, c:c + 1].to_broadcast([128, 8, 128]), [sc], [scb])

    def load_cast(dst3, src_ap, src_res, nk, width):
        P.dma("sp", wst[:, 0:nk, 0:width], src_ap.rearrange("(k p) n -> p k n", p=128), [src_res], [wst])
        h = nk // 2
        P.copy("pool", dst3[:, 0:h, :], wst[:, 0:h, 0:width], [wst], [dst3.tensor_res])
        P.copy("dve", dst3[:, h:nk, :], wst[:, h:nk, 0:width], [wst], [dst3.tensor_res])

    class V:
        def __init__(self, ap, res):
            self.ap = ap; self.tensor_res = res
        def __getitem__(self, k):
            return self.ap[k]

    for hf in range(2):
        P.dma("sp", wst[:], wg_d[:, hf * 512:(hf + 1) * 512].rearrange("(k p) n -> p k n", p=128), [wg_d], [wst])
        for c in range(2):
            pt = nps()
            for k in range(8):
                P.mm(pt[:], scb[:, c, k, :], wst[:, k, :], k == 0, k == 7, [scb, wst], [pt])
            P.tt("dve", gate[:, c, hf * 512:(hf + 1) * 512], gate[:, c, hf * 512:(hf + 1) * 512], pt[:], ALU.add, [gate, pt], [gate])
    load_cast(V(wglu[:], wglu), wglu_d[:, :], wglu_d, 4, 512)
    for i, wd in enumerate((wa_d, wb_d, wc_d)):
        for hf in range(2):
            load_cast(V(wbr[i][:, :, hf * 512:(hf + 1) * 512], wbr[i]), wd[:, hf * 512:(hf + 1) * 512], wd, 4, 512)
    for hf in range(2):
        load_cast(V(wout[:, :, hf * 512:(hf + 1) * 512], wout), wo_d[:, hf * 512:(hf + 1) * 512], wo_d, 8, 512)

    ld = [P.sbuf(f"ld{i}", [128, 512]) for i in range(6)]

    def load(src_ap, src_res):
        rr["l"] += 1
        t = ld[rr["l"] % 6]
        P.dma("sp", t[:, 0:src_ap.shape[1]], src_ap, [src_res], [t])
        return t
    ggb = P.sbuf("ggb", [128, 4, 512], BF16)
    sg = [P.sbuf(f"sg{i}", [128, 512]) for i in range(2)]
    sz = [P.sbuf(f"sz{i}", [128, 512]) for i in range(2)]
    n2 = {"n": 0}
    for (t0, tn) in TB:
        for ct in range(4):
            rs_ = slice(ct * 128, (ct + 1) * 128)
            a = load(yf_d[rs_, t0:t0 + tn], yf_d); b = load(yb_d[rs_, t0:t0 + tn], yb_d); u = load(ua_d[rs_, t0:t0 + tn], ua_d)
            P.tt("pool", a[:, 0:tn], a[:, 0:tn], b[:, 0:tn], ALU.add, [a, b], [a])
            P.stt("dve", a[:, 0:tn], u[:, 0:tn], dsk[:, ct:ct + 1], a[:, 0:tn], ALU.mult, ALU.add, [u, dsk, a], [a])
            P.act(gg[:, ct, 0:tn], a[:, 0:tn], AF.Gelu, [a], [gg])
            P.copy("pool", ggb[:, ct, 0:tn], gg[:, ct, 0:tn], [gg], [ggb])
        for co in range(4):
            i = n2["n"] % 2
            n2["n"] += 1
            pt = nps()
            for ci in range(4):
                P.mm(pt[:, 0:tn], wglu[:, ci, co * 128:(co + 1) * 128], ggb[:, ci, 0:tn], ci == 0, ci == 3, [wglu, ggb], [pt])
            P.act(sg[i][:, 0:tn], pt[:, 0:tn], AF.Sigmoid, [pt, bglu], [sg[i]], bias=bglu[:, co:co + 1], scale=1.0)
            z = load(z_d[co * 128:(co + 1) * 128, t0:t0 + tn], z_d)
            P.act(sz[i][:, 0:tn], z[:, 0:tn], AF.Silu, [z], [sz[i]])
            P.tt("dve", sg[i][:, 0:tn], sg[i][:, 0:tn], gg[:, co, 0:tn], ALU.mult, [sg[i], gg], [sg[i]])
            P.tt("pool", tbr[0][:, co, t0:t0 + tn], sg[i][:, 0:tn], sz[i][:, 0:tn], ALU.mult, [sg[i], sz[i]], [tbr[0]])
        for bi, (yd, zoff) in enumerate(((yn_d, 512), (ym_d, 1024))):
            for co in range(4):
                i = n2["n"] % 2
                n2["n"] += 1
                y = load(yd[co * 128:(co + 1) * 128, t0:t0 + tn], yd)
                z = load(z_d[zoff + co * 128:zoff + (co + 1) * 128, t0:t0 + tn], z_d)
                P.act(sz[i][:, 0:tn], z[:, 0:tn], AF.Silu, [z], [sz[i]])
                P.tt("dve" if co % 2 else "pool", tbr[1 + bi][:, co, t0:t0 + tn], y[:, 0:tn], sz[i][:, 0:tn], ALU.mult, [y, sz[i]], [tbr[1 + bi]])

    mT = [P.sbuf("mT", [128, 8, 512], BF16)] * 2
    sgt = [P.sbuf(f"sgt{i}", [128, 512]) for i in range(3)]
    acc = [P.sbuf(f"acc{i}", [128, 512]) for i in range(2)]
    def alias(i, name):
        t = Tile(wst.h[:, 2 * i:2 * i + 2, :].rearrange("p a b -> p (a b)"), name)
        t.res.last_w = wst.res.last_w
        t.res.readers = dict(wst.res.readers)
        return t
    xt = [alias(0, "xt0"), alias(1, "xt1")]
    rt = [alias(2, "rt0"), alias(3, "rt1")]
    st = [P.sbuf(f"st{i}", [128, 2, 6]) for i in range(2)]
    mv = [P.sbuf(f"mv{i}", [128, 4]) for i in range(2)]
    n3 = {"n": 0}
    for bi, (t0, tn) in enumerate(TB):
        MT = mT[bi % 2]
        for f in range(8):
            pts = []
            for br in range(3):
                pt = nps()
                for ci in range(4):
                    P.mm(pt[:, 0:tn], wbr[br][:, ci, f * 128:(f + 1) * 128], tbr[br][:, ci, t0:t0 + tn], ci == 0, ci == 3,
                         [wbr[br], tbr[br]], [pt])
                pts.append(pt)
            A = acc[f % 2]
            for br in range(3):
                g = load(g_d[br * 1024 + f * 128:br * 1024 + (f + 1) * 128, t0:t0 + tn], g_d)
                P.act(sgt[br][:, 0:tn], g[:, 0:tn], AF.Sigmoid, [g], [sgt[br]])
                P.tt("dve", sgt[br][:, 0:tn], sgt[br][:, 0:tn], pts[br][:, 0:tn], ALU.mult, [sgt[br], pts[br]], [sgt[br]])
            P.tt("pool", A[:, 0:tn], sgt[0][:, 0:tn], sgt[1][:, 0:tn], ALU.add, [sgt[0], sgt[1]], [A])
            P.tt("pool", MT[:, f, 0:tn], A[:, 0:tn], sgt[2][:, 0:tn], ALU.add, [A, sgt[2]], [MT])
        for tt_ in range(tn // 128):
            i = n3["n"] % 2
            n3["n"] += 1
            tok0 = t0 + tt_ * 128
            c = 0 if tok0 < 2048 else 1
            X, R, ST, MV = xt[i], rt[i], st[i], mv[i]
            P.dma("sp", X[:], x_d[tok0:tok0 + 128, :], [x_d], [X])
            for hf in range(2):
                pt = nps()
                for k in range(8):
                    P.mm(pt[:], MT[:, k, tt_ * 128:(tt_ + 1) * 128], wout[:, k, hf * 512:(hf + 1) * 512], k == 0, k == 7, [MT, wout], [pt])
                hs = slice(hf * 512, (hf + 1) * 512)
                P.tt("dve", R[:, hs], gate[:, c, hs], pt[:], ALU.mult, [gate, pt], [R])
                P.stt("dve", R[:, hs], X[:, hs], ALPHA, R[:, hs], ALU.mult, ALU.add, [X, R], [R])
            P.op("dve", lambda ST=ST, R=R: nc.vector.bn_stats(ST[:, 0, :], R[:, 0:512]), [R], [ST])
            P.op("dve", lambda ST=ST, R=R: nc.vector.bn_stats(ST[:, 1, :], R[:, 512:1024]), [R], [ST])
            P.op("dve", lambda ST=ST, MV=MV: nc.vector.bn_aggr(MV[:, 0:2], ST[:]), [ST], [MV])
            P.act(MV[:, 2:3], MV[:, 1:2], AF.Sqrt, [MV], [MV], bias=LN_EPS, scale=1.0)
            P.op("dve", lambda MV=MV: nc.vector.reciprocal(MV[:, 3:4], MV[:, 2:3]), [MV], [MV])
            P.ts("dve", R[:], R[:], MV[:, 0:1], MV[:, 3:4], ALU.subtract, ALU.mult, [R, MV], [R])
            P.tt("pool", R[:], R[:], lng[:], ALU.mult, [R, lng], [R])
            P.tt("pool", X[:], R[:], lnb[:], ALU.add, [R, lnb], [X])
            P.dma("pool", xo[tok0:tok0 + 128, :], X[:], [X], [xo])
    P.fence("pool", [xo])
    P.emit()
    return P


_PROGS = {}


def _prog(name, builder):
    if name not in _PROGS:
        _PROGS[name] = builder()
    return _PROGS[name]


def _launch(P, maps):
    res = run_bass_kernel_spmd(P.nc, maps, core_ids=list(range(8)))
    return [{k: np.asarray(v) for k, v in r.items()} for r in res.results]


def kernel(**inputs):
    inp = {k: np.asarray(v) for k, v in inputs.items()}
    x_cur = np.array(inp['x'], dtype=np.float32, copy=True)
    ctx_cur = np.array(inp['ctx'], dtype=np.float32, copy=True)
    for l in range(4):
        r1 = _launch(_prog("l1", build_l1), l1_inputs(inp, l, x_cur, ctx_cur))
        ua_lat, ua_ctx = ua_from_l1(r1)
        r2 = _launch(_prog("s5", build_s5), s5_inputs(inp, l, ua_lat, ua_ctx))
        yf, yb = s5_gather(r2)
        r3 = _launch(_prog("attn", build_attn), attn_inputs(inp, l, r1))
        r4 = _launch(_prog("l3", build_l3), l3_inputs(inp, l, x_cur, ctx_cur, r1, yf, yb, r3))
        for c in range(8):
            b, j = c // 4, c % 4
            x_cur[b, j * 2048:(j + 1) * 2048] = r4[c]['xo'][:2048]
            if j == 0:
                ctx_cur[b] = r4[c]['xo'][2048:]
    return x_cur
```

```python
import numpy as np
import concourse.bass as bass
import concourse.mybir as mybir
from concourse.bass_utils import run_bass_kernel_spmd

F32 = mybir.dt.float32
BF16 = mybir.dt.bfloat16
ALU = mybir.AluOpType
AF = mybir.ActivationFunctionType
AX = mybir.AxisListType


class Res:
    __slots__ = ("name", "last_w", "readers")

    def __init__(self, name):
        self.name = name
        self.last_w = None
        self.readers = {}


class Tile:
    def __init__(self, h, name):
        self.h = h
        self.res = Res(name)

    def __getitem__(self, k):
        return self.h[k]


def _res(x):
    return x.res if isinstance(x, Tile) else x


class Prog:
    ENG = ("pe", "act", "dve", "pool", "sp")
    RING = 16

    def __init__(self):
        self.nc = bass.Bass("TRN2", target_bir_lowering=False)
        self.q = {e: [] for e in self.ENG}
        self.cnt = {e: 0 for e in self.ENG}
        self.dman = {e: 0 for e in self.ENG}
        self.waited = {e: {} for e in self.ENG}
        self.sems = {}
        self.ntile = 0

    def eng(self, e):
        nc = self.nc
        return {"pe": nc.tensor, "act": nc.scalar, "dve": nc.vector, "pool": nc.gpsimd, "sp": nc.sync}[e]

    def sbuf(self, name, shape, dtype=F32):
        self.ntile += 1
        return Tile(self.nc.alloc_sbuf_tensor(f"{name}_{self.ntile}", list(shape), dtype), name)

    def psum(self, name, shape, dtype=F32):
        self.ntile += 1
        return Tile(self.nc.alloc_psum_tensor(f"{name}_{self.ntile}", list(shape), dtype), name)

    def dram_in(self, name, shape, dtype=F32):
        return Tile(self.nc.dram_tensor(name, list(shape), dtype, kind="ExternalInput"), name)

    def dram_out(self, name, shape, dtype=F32):
        return Tile(self.nc.dram_tensor(name, list(shape), dtype, kind="ExternalOutput"), name)

    def dram_tmp(self, name, shape, dtype=F32):
        return Tile(self.nc.dram_tensor(name, list(shape), dtype, kind="Internal"), name)

    def op(self, e, fn, reads=(), writes=(), dma=False):
        toks = {}

        def add(t):
            if t is None:
                return
            k, v = t
            if toks.get(k, 0) < v:
                toks[k] = v

        reads = [_res(r) for r in reads]
        writes = [_res(w) for w in writes]
        for r in reads:
            add(r.last_w)
        for w in writes:
            add(w.last_w)
            for k, v in w.readers.items():
                add((k, v))
        if fn is None:
            tok = None
        elif dma:
            n = self.dman[e]
            self.dman[e] += 1
            slot, use = n % self.RING, n // self.RING
            key = ("d", e, slot)
            if use > 0:
                add((key, 16 * use))
            tok = (key, 16 * (use + 1))
        else:
            self.cnt[e] += 1
            tok = (("c", e), self.cnt[e])
        waits = []
        wd = self.waited[e]
        for k, v in toks.items():
            if e == "pe" and k == ("c", "pe"):
                continue
            if wd.get(k, 0) >= v:
                continue
            wd[k] = v
            waits.append((k, v))
        self.q[e].append((waits, fn, tok))
        if tok is not None:
            for r in reads:
                if r.readers.get(tok[0], 0) < tok[1]:
                    r.readers[tok[0]] = tok[1]
            for w in writes:
                w.last_w = tok
                w.readers = {}
        return tok

    def fence(self, e, res):
        self.op(e, None, reads=res, writes=res)

    def _sem(self, key):
        if key not in self.sems:
            self.sems[key] = self.nc.alloc_semaphore(name="s_" + "_".join(str(x) for x in key))
        return self.sems[key]

    def emit(self):
        nc = self.nc
        for e in self.ENG:
            for waits, fn, tok in self.q[e]:
                for k, v in waits:
                    self._sem(k)
                if tok is not None:
                    self._sem(tok[0])

        def replay(e):
            eng = self.eng(e)
            for waits, fn, tok in self.q[e]:
                for k, v in waits:
                    eng.wait_ge(self.sems[k], v)
                if fn is not None:
                    ins = fn()
                    ins.then_inc(self.sems[tok[0]], 16 if tok[0][0] == "d" else 1)

        with nc.Block() as block:
            @block.tensor
            def _(x):
                replay("pe")

            @block.scalar
            def _(x):
                replay("act")

            @block.vector
            def _(x):
                replay("dve")

            @block.gpsimd
            def _(x):
                replay("pool")

            @block.sync
            def _(x):
                replay("sp")
        return nc

    def dma(self, e, out, in_, reads, writes, **kw):
        eng = self.eng(e)
        self.op(e, lambda: eng.dma_start(out=out, in_=in_, **kw), reads, writes, dma=True)

    def mm(self, out, lhsT, rhs, start, stop, reads, writes):
        nc = self.nc
        self.op("pe", lambda: nc.tensor.matmul(out, lhsT, rhs, start=start, stop=stop), reads, writes)

    def tr(self, out, in_, ident, reads, writes):
        nc = self.nc
        self.op("pe", lambda: nc.tensor.transpose(out, in_, ident), reads, writes)

    def act(self, out, in_, func, reads, writes, bias=None, scale=None, accum_out=None, e="act"):
        nc = self.nc
        kw = {}
        if bias is not None:
            kw["bias"] = bias
        if scale is not None:
            kw["scale"] = scale
        if accum_out is not None:
            kw["accum_out"] = accum_out
        self.op("act", lambda: nc.scalar.activation(out, in_, func, **kw), reads, writes)

    def ts(self, e, out, in0, s1, s2, op0, op1, reads, writes, accum_out=None):
        eng = self.eng(e)
        if op1 is None:
            self.op(e, lambda: eng.tensor_scalar(out, in0, s1, None, op0), reads, writes)
        elif accum_out is not None:
            self.op(e, lambda: eng.tensor_scalar(out, in0, s1, s2, op0, op1, accum_out), reads, writes)
        else:
            self.op(e, lambda: eng.tensor_scalar(out, in0, s1, s2, op0, op1), reads, writes)

    def tt(self, e, out, in0, in1, op, reads, writes):
        eng = self.eng(e)
        self.op(e, lambda: eng.tensor_tensor(out, in0, in1, op), reads, writes)

    def stt(self, e, out, in0, scalar, in1, op0, op1, reads, writes):
        eng = self.eng(e)
        self.op(e, lambda: eng.scalar_tensor_tensor(out, in0, scalar, in1, op0, op1), reads, writes)

    def copy(self, e, out, in_, reads, writes):
        eng = self.eng(e)
        if e == "act":
            self.op(e, lambda: eng.copy(out, in_), reads, writes)
        else:
            self.op(e, lambda: eng.tensor_copy(out, in_), reads, writes)

    def memset(self, e, ap, val, writes):
        eng = self.eng(e)
        self.op(e, lambda: eng.memset(ap, val), (), writes)


def run(P, in_maps, trace=False):
    return run_bass_kernel_spmd(P.nc, in_maps, core_ids=list(range(len(in_maps))), trace=trace)


import ml_dtypes
BF = ml_dtypes.bfloat16
SWAP = np.concatenate([np.arange(8, 16), np.arange(0, 8), np.arange(24, 32), np.arange(16, 24)])

def rope_tables():
    t = np.arange(8192)
    row = (t // 64).astype(np.float32); col = (t % 64).astype(np.float32)
    inv = (1.0 / (np.float32(10000.0) ** (np.arange(0, 16, 2, dtype=np.float32) / np.float32(16)))).astype(np.float32)
    ar = row[:, None] * inv[None, :]; ac = col[:, None] * inv[None, :]
    cr, sr, cc, sn = np.cos(ar), np.sin(ar), np.cos(ac), np.sin(ac)
    C32 = np.concatenate([cr, cr, cc, cc], axis=1).T.astype(np.float32)
    S32 = np.concatenate([-sr, sr, -sn, sn], axis=1).T.astype(np.float32)
    return C32, S32

def l1_inputs(inp, l, x_cur, ctx_cur):
    C32, S32 = rope_tables()
    w_in = inp['w_in'][l]
    w_uq = inp['mla_w_uq'][l]
    w_uq2 = w_uq.reshape(256, 8, 96).copy()
    w_uq2[:, :, 64:] = w_uq2[:, :, 64:][:, :, SWAP]
    common = dict(
        w_mod=inp['w_mod'][l], b_mod=np.ascontiguousarray(inp['b_mod'][l].reshape(24, 128).T),
        w_in=w_in, w_krs=np.ascontiguousarray(w_in[:, 3456 + SWAP]),
        qn=np.ascontiguousarray(inp['mla_q_norm'][l].reshape(2, 128).T), kvn=np.ascontiguousarray(inp['mla_kv_norm'][l].reshape(1, 128).T),
        w_uq=w_uq, w_uq2=np.ascontiguousarray(w_uq2.reshape(256, 768)), w_ukv=inp['mla_w_ukv'][l],
        ident=np.eye(128, dtype=np.float32))
    maps = []
    for c in range(8):
        b, j = c // 4, c % 4
        x = np.concatenate([x_cur[b, j * 2048:(j + 1) * 2048], ctx_cur[b]], axis=0)
        csc = np.zeros((128, 8, 2), np.float32)
        csc[:, :, 0] = inp['c'][b].reshape(8, 128).T
        csc[:, :, 1] = inp['c_ctx'].reshape(8, 128).T
        c32 = np.concatenate([C32[:, j * 2048:(j + 1) * 2048], np.ones((32, 256), np.float32)], axis=1)
        s32 = np.concatenate([S32[:, j * 2048:(j + 1) * 2048], np.zeros((32, 256), np.float32)], axis=1)
        c96 = np.concatenate([np.ones((64, 2304), np.float32), c32], axis=0)
        s96 = np.concatenate([np.zeros((64, 2304), np.float32), s32], axis=0)
        m = dict(common); m.update(x=np.ascontiguousarray(x), csc=csc.reshape(128, 16), c96=c96, s96=s96, c32=c32, s32=s32)
        maps.append(m)
    return maps

def s5_inputs(inp, l, ua_lat, ua_ctx):
    B = ua_lat.shape[0]
    seqs = [[np.concatenate([ua_ctx[b], ua_lat[b]], 0) for b in range(B)],
            [np.concatenate([ua_ctx[b][::-1], ua_lat[b][::-1]], 0) for b in range(B)]]
    tm = np.zeros((8, 16, 8, 16), np.float32)
    for sp in range(8):
        for t in range(8):
            if t >= 7 - sp:
                tm[sp, :, t, :] = 1.0
    tm = tm.reshape(128, 128)
    maps = []
    for c in range(8):
        u = np.empty((4, 2, 2, 128, 1056), np.float32)
        par = np.empty((128, 12), np.float32)
        bc = np.empty((128, 4, 4, 16), np.float32)
        for q in range(4):
            g = 4 * c + q
            for d in range(2):
                sl = slice(d * 64, d * 64 + 64)
                par[sl, q] = inp['ssm_lam_re'][l, d, g]
                par[sl, 4 + q] = inp['ssm_lam_im'][l, d, g]
                par[sl, 8 + q] = inp['ssm_log_dt'][l, d, g]
                bc[sl, q, 0] = inp['ssm_b_re'][l, d, g]
                bc[sl, q, 1] = inp['ssm_b_im'][l, d, g]
                bc[sl, q, 2] = inp['ssm_c_re'][l, d, g].T
                bc[sl, q, 3] = inp['ssm_c_im'][l, d, g].T
                for b in range(B):
                    X = seqs[d][b][:, 16 * g:16 * g + 16].reshape(1056, 8, 16)[:, ::-1, :]
                    u[q, d, b] = X.transpose(1, 2, 0).reshape(128, 1056)
        maps.append(dict(u=u, par=par, bc=bc, tmask=tm, ident=np.eye(128, dtype=np.float32)))
    return maps

def s5_gather(results, B=2):
    yf = np.empty((B, 8448, 512), np.float32); yb = np.empty((B, 8448, 512), np.float32)
    for c in range(8):
        y = results[c]["y"]
        for q in range(4):
            g = 4 * c + q
            for b in range(B):
                for d in range(2):
                    Y = y[q, d, b].reshape(8, 16, 1056).transpose(2, 0, 1).reshape(8448, 16)
                    if d == 0:
                        yf[b, :, 16 * g:16 * g + 16] = Y
                    else:
                        yb[b, :256, 16 * g:16 * g + 16] = Y[:256][::-1]
                        yb[b, 256:, 16 * g:16 * g + 16] = Y[256:][::-1]
    return yf, yb

NEG = -30000.0

def _u16(a):
    return np.ascontiguousarray(a)

def attn_consts(rpb_l):
    kc = np.arange(64)[:, None]; qc = np.arange(64)[None, :]
    cs = np.clip(qc - 8, 0, 48)
    colvalid = (kc >= cs) & (kc < cs + 16)
    ci = np.clip(kc - qc, -15, 15) + 15
    gp = np.full((8, 2, 64, 22, 64), NEG, np.float32)
    for kl in range(2):
        for e in range(22):
            dr = kl + 10 - e
            if abs(dr) <= 7:
                vals = rpb_l[:, dr + 7, :][:, ci]
                gp[:, kl, :, e, :] = np.where(colvalid[None], vals, np.float32(NEG))
    gp = gp.reshape(8, 128, 22 * 64)
    bq = np.zeros((8, 8, 64), np.float32)
    for r in range(8):
        bq[r, r, :] = 1.0
    return gp, bq.reshape(8, 512).astype(BF)

def attn_rowmask(j):
    am = np.full((8, 4, 8, 2, 64), NEG, np.float32)
    for blk in range(4):
        for ql in range(8):
            qr = 32 * j + 8 * blk + ql
            ws = int(np.clip(qr - 4, 0, 120))
            for i in range(8):
                for kl in range(2):
                    kr = 32 * j + 8 * blk - 4 + 2 * i + kl
                    if 0 <= kr < 128 and ws <= kr < ws + 8:
                        am[ql, blk, i, kl, :] = 0.0
    return am.reshape(8, 32 * 128).astype(BF)

def attn_inputs(inp, l, r1):
    gp, bq = attn_consts(inp['na_rpb'][l])
    one = np.ones((), BF)
    maps = []
    per_b = {}
    for b in range(2):
        cores = [r1[4 * b + j] for j in range(4)]
        klat = np.concatenate([c['mknT'][:, :2048] for c in cores], axis=1).reshape(8, 64, 8192)
        kctx = cores[0]['mknT'][:, 2048:].reshape(8, 64, 256)
        kpe = np.concatenate([c['mkpe'][:, :2048] for c in cores] + [cores[0]['mkpe'][:, 2048:]], axis=1)
        mk = np.concatenate([np.concatenate([klat, kctx], axis=2), np.broadcast_to(kpe[None], (8, 32, 8448))], axis=1)
        v = np.concatenate([c['mv'][:2048] for c in cores] + [cores[0]['mv'][2048:]], axis=0).reshape(8448, 8, 64)
        va = np.concatenate([v, np.ones((8448, 8, 1), BF)], axis=2)
        mv = va.reshape(66, 128, 8, 65).transpose(2, 1, 0, 3).reshape(8, 128, 66 * 65)
        nkl = np.concatenate([c['nakT'][:, :2048] for c in cores], axis=1).reshape(8, 64, 8192)
        nkc = cores[0]['nakT'][:, 2048:].reshape(8, 64, 256)
        nvl = np.concatenate([c['nav'][:2048] for c in cores], axis=0)
        nvc = cores[0]['nav'][2048:]
        per_b[b] = (np.ascontiguousarray(mk), np.ascontiguousarray(mv), nkl, nkc, nvl, nvc)
    for c in range(8):
        b, j = c // 4, c % 4
        mk, mv, nkl, nkc, nvl, nvc = per_b[b]
        t0 = (32 * j - 4) * 64
        nk = np.zeros((8, 64, 2816), BF); nvv = np.zeros((2816, 512), BF)
        lo, hi = max(t0, 0), min(t0 + 2560, 8192)
        nk[:, :, lo - t0:hi - t0] = nkl[:, :, lo:hi]
        nk[:, :, 2560:] = nkc
        nvv[lo - t0:hi - t0] = nvl[lo:hi]
        nvv[2560:] = nvc
        nva = np.concatenate([nvv.reshape(2816, 8, 64), np.ones((2816, 8, 1), BF)], axis=2)
        nv = nva.reshape(22, 128, 8, 65).transpose(2, 1, 0, 3).reshape(8, 128, 22 * 65)
        maps.append(dict(mq=r1[c]['mqT'], mk=mk, mv=mv, nq=np.ascontiguousarray(r1[c]['naqT'].reshape(8, 64, 2304)),
                         nk=nk, nv=np.ascontiguousarray(nv), gp=gp, am=attn_rowmask(j), bq=bq))
    return maps

def l3_inputs(inp, l, x_cur, ctx_cur, r1, yf, yb, r3):
    bc = lambda v: np.ascontiguousarray(np.broadcast_to(v[None, :], (128, v.shape[0])))
    pl = lambda v: np.ascontiguousarray(v.reshape(-1, 128).T)
    common = dict(dskip=pl(inp['ssm_d'][l]), bglu=pl(inp['ssm_b_glu'][l]), w_glu=inp['ssm_w_glu'][l],
                  w_a=inp['w_branch_a'][l], w_b=inp['w_branch_b'][l], w_c=inp['w_branch_c'][l], w_o=inp['w_out'][l],
                  w_gate=np.ascontiguousarray(inp['w_mod'][l][:, 2048:3072]), b_gate=bc(inp['b_mod'][l][2048:3072]),
                  ln_g=bc(inp['ln_g'][l]), ln_b=bc(inp['ln_b'][l]))
    maps = []
    for c in range(8):
        b, j = c // 4, c % 4
        x = np.concatenate([x_cur[b, j * 2048:(j + 1) * 2048], ctx_cur[b]], axis=0)
        csc = np.zeros((128, 8, 2), np.float32)
        csc[:, :, 0] = inp['c'][b].reshape(8, 128).T
        csc[:, :, 1] = inp['c_ctx'].reshape(8, 128).T
        own = lambda y: np.ascontiguousarray(np.concatenate([y[b, 256 + j * 2048:256 + (j + 1) * 2048], y[b, :256]], axis=0).T)
        m = dict(common)
        m.update(x=np.ascontiguousarray(x), csc=csc.reshape(128, 16), yfT=own(yf), ybT=own(yb), uaT=r1[c]['uaT'], zT=r1[c]['zT'],
                 gT=r1[c]['gT'], ynT=r3[c]['ynT'], ymT=r3[c]['ymT'])
        maps.append(m)
    return maps

def ua_from_l1(r1):
    ua_lat = np.empty((2, 8192, 512), np.float32); ua_ctx = np.empty((2, 256, 512), np.float32)
    for c in range(8):
        b, j = c // 4, c % 4
        ua_lat[b, j * 2048:(j + 1) * 2048] = r1[c]['uaT'][:, :2048].T
        if j == 0:
            ua_ctx[b] = r1[c]['uaT'][:, 2048:].T
    return ua_lat, ua_ctx


NTOK = 2304
NLAT = 2048
TB = [(0, 512), (512, 512), (1024, 512), (1536, 512), (2048, 256)]
LN_EPS = 1e-5
RMS_EPS = 1e-6


def build_l1():
    P = Prog()
    nc = P.nc
    x_d = P.dram_in("x", [NTOK, 1024])
    csc_d = P.dram_in("csc", [128, 16])
    wmod_d = P.dram_in("w_mod", [1024, 3072])
    bmod_d = P.dram_in("b_mod", [128, 24])
    win_d = P.dram_in("w_in", [1024, 7072])
    wkrs_d = P.dram_in("w_krs", [1024, 32])
    qn_d = P.dram_in("qn", [128, 2])
    kvn_d = P.dram_in("kvn", [128, 1])
    wuq_d = P.dram_in("w_uq", [256, 768])
    wuq2_d = P.dram_in("w_uq2", [256, 768])
    wukv_d = P.dram_in("w_ukv", [128, 1024])
    c96_d = P.dram_in("c96", [96, NTOK])
    s96_d = P.dram_in("s96", [96, NTOK])
    c32_d = P.dram_in("c32", [32, NTOK])
    s32_d = P.dram_in("s32", [32, NTOK])
    idd = P.dram_in("ident", [128, 128])

    uaT_o = P.dram_out("uaT", [512, NTOK])
    zT_o = P.dram_out("zT", [1536, NTOK])
    g3_o = P.dram_out("gT", [3072, NTOK])
    naq_o = P.dram_out("naqT", [512, NTOK], BF16)
    nak_o = P.dram_out("nakT", [512, NTOK], BF16)
    nav_o = P.dram_out("nav", [NTOK, 512], BF16)
    mq_o = P.dram_out("mqT", [8, 96, NTOK], BF16)
    mkn_o = P.dram_out("mknT", [512, NTOK], BF16)
    mkpe_o = P.dram_out("mkpe", [32, NTOK], BF16)
    mv_o = P.dram_out("mv", [NTOK, 512], BF16)
    outs = [uaT_o, zT_o, g3_o, naq_o, nak_o, nav_o, mq_o, mkn_o, mkpe_o, mv_o]

    identf = P.sbuf("identf", [128, 128])
    ident = P.sbuf("ident", [128, 128], BF16)
    ones = P.sbuf("ones", [128, 128], BF16)
    hT = P.sbuf("hT", [128, 8, NTOK], BF16)
    wst = [P.sbuf(f"wst{i}", [128, 8, 512]) for i in range(2)]
    wbf = [P.sbuf(f"wbf{i}", [128, 8, 512], BF16) for i in range(2)]
    csc = P.sbuf("csc", [128, 16])
    sc = P.sbuf("sc", [128, 16])
    bmod = P.sbuf("bmod", [128, 24])
    mods = P.sbuf("mods", [128, 16, 2])
    xt = [P.sbuf(f"xt{i}", [128, 1024]) for i in range(2)]
    xh = [P.sbuf(f"xh{i}", [128, 1024], BF16) for i in range(2)]
    st = [P.sbuf(f"st{i}", [128, 2, 6]) for i in range(2)]
    mv = [P.sbuf(f"mv{i}", [128, 4]) for i in range(2)]
    cmst = [P.sbuf(f"cmst{i}", [128, NTOK]) for i in range(2)]
    cmsb = [P.sbuf(f"cmsb{i}", [128, NTOK], BF16) for i in range(2)]
    tmst = [P.sbuf(f"tmst{i}", [128, 512]) for i in range(2)]
    tmsb = [P.sbuf(f"tmsb{i}", [128, 512], BF16) for i in range(2)]
    psA = [P.psum(f"psA{i}", [128, 512]) for i in range(4)]
    psT = [P.psum(f"psT{i}", [128, 1024], BF16) for i in range(2)]
    psM = P.psum("psM", [128, 512])
    psB = P.psum("psB", [128, 512])
    rr = {"ps": 0, "ev": 0}

    def next_ps():
        rr["ps"] += 1
        return psA[rr["ps"] % 4]

    def evac(out, in_, reads, writes):
        rr["ev"] += 1
        if rr["ev"] % 2:
            P.copy("act", out, in_, reads, writes)
        else:
            P.copy("dve", out, in_, reads, writes)

    P.dma("sp", identf[:], idd[:], [idd], [identf])
    P.copy("dve", ident[:], identf[:], [identf], [ident])
    P.memset("pool", ones[:], 1.0, [ones])
    P.dma("sp", csc[:], csc_d[:], [csc_d], [csc])
    P.dma("sp", bmod[:], bmod_d[:], [bmod_d], [bmod])
    P.act(sc[:], csc[:], AF.Silu, [csc], [sc])

    for ch in range(4):
        w = wst[ch % 2]
        P.dma("sp", w[:], wmod_d[:, ch * 512:(ch + 1) * 512].rearrange("(k p) n -> p k n", p=128), [wmod_d], [w])
        for j in range(4):
            f = ch * 4 + j
            for k in range(8):
                P.mm(psM[:, 2 * f:2 * f + 2], w[:, k, j * 128:(j + 1) * 128], sc[:, 2 * k:2 * k + 2],
                     k == 0, k == 7, [w, sc], [psM])
    P.tt("dve", mods[:], psM[:, 0:32].rearrange("p (f c) -> p f c", c=2),
         bmod[:, 0:16].unsqueeze(2).to_broadcast([128, 16, 2]), ALU.add, [psM, bmod], [mods])
    P.ts("dve", mods[:, 8:16, :], mods[:, 8:16, :], 1.0, None, ALU.add, None, [mods], [mods])

    for t in range(18):
        X, XH, ST, MV = xt[t % 2], xh[t % 2], st[t % 2], mv[t % 2]
        P.dma("sp", X[:], x_d[t * 128:(t + 1) * 128, :], [x_d], [X])
        P.op("dve", lambda ST=ST, X=X: nc.vector.bn_stats(ST[:, 0, :], X[:, 0:512]), [X], [ST])
        P.op("dve", lambda ST=ST, X=X: nc.vector.bn_stats(ST[:, 1, :], X[:, 512:1024]), [X], [ST])
        P.op("dve", lambda ST=ST, MV=MV: nc.vector.bn_aggr(MV[:, 0:2], ST[:]), [ST], [MV])
        P.act(MV[:, 2:3], MV[:, 1:2], AF.Sqrt, [MV], [MV], bias=LN_EPS, scale=1.0)
        P.op("dve", lambda MV=MV: nc.vector.reciprocal(MV[:, 3:4], MV[:, 2:3]), [MV], [MV])
        P.ts("dve", XH[:], X[:], MV[:, 0:1], MV[:, 3:4], ALU.subtract, ALU.mult, [X, MV], [XH])
        pt = psT[t % 2]
        for k in range(8):
            P.tr(pt[:, k * 128:(k + 1) * 128], XH[:, k * 128:(k + 1) * 128], ident[:], [XH, ident], [pt])
        c = 0 if t < 16 else 1
        for k in range(8):
            P.act(hT[:, k, t * 128:(t + 1) * 128], pt[:, k * 128:(k + 1) * 128], AF.Identity, [pt, mods], [hT],
                  bias=mods[:, k, c:c + 1], scale=mods[:, 8 + k, c:c + 1])

    wn = {"n": 0}

    def load_w(src_ap, src_res, width):
        i = wn["n"] % 2
        wn["n"] += 1
        P.dma("sp", wst[i][:, :, 0:width], src_ap.rearrange("(k p) n -> p k n", p=128), [src_res], [wst[i]])
        P.copy("pool", wbf[i][:, 0:4, 0:width], wst[i][:, 0:4, 0:width], [wst[i]], [wbf[i]])
        P.copy("dve", wbf[i][:, 4:8, 0:width], wst[i][:, 4:8, 0:width], [wst[i]], [wbf[i]])
        return wbf[i]

    cmn = {"n": 0}

    def proj_cm(w, width, dst_ap, dst_res, bf):
        for j in range(0, width, 128):
            m = min(128, width - j)
            i = cmn["n"] % 2
            cmn["n"] += 1
            stg = cmsb[i] if bf else cmst[i]
            for (t0, tn) in TB:
                ps = next_ps()
                for k in range(8):
                    P.mm(ps[0:m, 0:tn], w[:, k, j:j + m], hT[:, k, t0:t0 + tn], k == 0, k == 7, [w, hT], [ps])
                evac(stg[0:m, t0:t0 + tn], ps[0:m, 0:tn], [ps], [stg])
            P.dma("pool", dst_ap[j:j + m, :], stg[0:m, :], [stg], [dst_res])

    tmn = {"n": 0}

    def proj_tm(w, dst_ap, dst_res, bf):
        for t in range(18):
            i = tmn["n"] % 2
            tmn["n"] += 1
            stg = tmsb[i] if bf else tmst[i]
            ps = next_ps()
            for k in range(8):
                P.mm(ps[:, :], hT[:, k, t * 128:(t + 1) * 128], w[:, k, :], k == 0, k == 7, [w, hT], [ps])
            evac(stg[:], ps[:], [ps], [stg])
            P.dma("pool", dst_ap[t * 128:(t + 1) * 128, :], stg[:], [stg], [dst_res])

    def wcols(c0, width):
        return win_d[:, c0:c0 + width]

    w = load_w(wcols(0, 512), win_d, 512);    proj_cm(w, 512, uaT_o[:], uaT_o, False)
    w = load_w(wcols(512, 512), win_d, 512);  proj_cm(w, 512, zT_o[0:512, :], zT_o, False)
    w = load_w(wcols(1024, 512), win_d, 512); proj_cm(w, 512, naq_o[:], naq_o, True)
    w = load_w(wcols(1536, 512), win_d, 512); proj_cm(w, 512, nak_o[:], nak_o, True)
    w = load_w(wcols(2048, 512), win_d, 512); proj_tm(w, nav_o[:], nav_o, True)
    w = load_w(wcols(2560, 512), win_d, 512); proj_cm(w, 512, zT_o[512:1024, :], zT_o, False)
    w = load_w(wcols(3488, 512), win_d, 512); proj_cm(w, 512, zT_o[1024:1536, :], zT_o, False)
    for g in range(6):
        w = load_w(wcols(4000 + g * 512, 512), win_d, 512)
        proj_cm(w, 512, g3_o[g * 512:(g + 1) * 512, :], g3_o, False)

    wm = load_w(wcols(3072, 416), win_d, 416)
    wk = load_w(wkrs_d[:, :], wkrs_d, 32)
    uqst = cmst[0]; uq = P.sbuf("uq", [128, 2, 768], BF16)
    uq2st = cmst[1]; uq2 = P.sbuf("uq2", [128, 2, 768], BF16)
    ukvst = xt[0]; ukvk = P.sbuf("ukvk", [128, 8, 64], BF16); ukvv = P.sbuf("ukvv", [128, 8, 64], BF16)
    qn = P.sbuf("qn", [128, 2]); kvn = P.sbuf("kvn", [128, 1])
    P.dma("sp", uqst[:, 0:1536].rearrange("p (k n) -> p k n", k=2), wuq_d[:].rearrange("(k p) n -> p k n", p=128), [wuq_d], [uqst])
    P.dma("sp", uq2st[:, 0:1536].rearrange("p (k n) -> p k n", k=2), wuq2_d[:].rearrange("(k p) n -> p k n", p=128), [wuq2_d], [uq2st])
    P.dma("sp", ukvst[:], wukv_d[:], [wukv_d], [ukvst])
    P.dma("sp", qn[:], qn_d[:], [qn_d], [qn])
    P.dma("sp", kvn[:], kvn_d[:], [kvn_d], [kvn])
    for k in range(2):
        P.ts("dve", uq[:, k, :], uqst[:, k * 768:(k + 1) * 768], qn[:, k:k + 1], None, ALU.mult, None, [uqst, qn], [uq])
        P.ts("dve", uq2[:, k, :], uq2st[:, k * 768:(k + 1) * 768], qn[:, k:k + 1], None, ALU.mult, None, [uq2st, qn], [uq2])
    u3 = ukvst[:].rearrange("p (h c) -> p h c", c=128)
    P.ts("dve", ukvk[:], u3[:, :, 0:64], kvn[:, 0:1], None, ALU.mult, None, [ukvst, kvn], [ukvk])
    P.ts("dve", ukvv[:], u3[:, :, 64:128], kvn[:, 0:1], None, ALU.mult, None, [ukvst, kvn], [ukvv])

    cq = [P.sbuf(f"cq{i}", [128, 2, 512]) for i in range(2)]
    ckv = [P.sbuf(f"ckv{i}", [128, 512]) for i in range(2)]
    kr = [P.sbuf(f"kr{i}", [32, 2, 512]) for i in range(2)]
    sq = [P.sbuf("sq", [128, 3, 512], BF16)] * 2
    rs = [P.sbuf("rs", [128, 2, 512])] * 2
    cqn = [P.sbuf(f"cqn{i}", [128, 2, 512], BF16) for i in range(2)]
    ckvn = [P.sbuf(f"ckvn{i}", [128, 512], BF16) for i in range(2)]
    tb96 = [P.sbuf("tb96", [96, 2, 512])] * 2
    tb32 = [P.sbuf("tb32", [32, 2, 512])] * 2
    t1 = [P.sbuf(f"t1{i}", [96, 512]) for i in range(2)]
    t2 = [P.sbuf(f"t2{i}", [96, 512]) for i in range(2)]
    qo = [P.sbuf(f"qo{i}", [96, 512], BF16) for i in range(2)]
    kno = [P.sbuf(f"kno{i}", [128, 512], BF16) for i in range(2)]
    kpo = [P.sbuf(f"kpo{i}", [32, 512], BF16) for i in range(2)]
    vo = [P.sbuf(f"vo{i}", [128, 512], BF16) for i in range(2)]
    nq = {"n": 0}
    for bi, (t0, tn) in enumerate(TB):
        b2 = bi % 2
        CQ, CKV, KR, SQ, RS, CQN, CKVN, T96, T32 = cq[b2], ckv[b2], kr[b2], sq[b2], rs[b2], cqn[b2], ckvn[b2], tb96[b2], tb32[b2]
        P.dma("sp", T96[:, 0, 0:tn], c96_d[:, t0:t0 + tn], [c96_d], [T96])
        P.dma("sp", T96[:, 1, 0:tn], s96_d[:, t0:t0 + tn], [s96_d], [T96])
        P.dma("sp", T32[:, 0, 0:tn], c32_d[:, t0:t0 + tn], [c32_d], [T32])
        P.dma("sp", T32[:, 1, 0:tn], s32_d[:, t0:t0 + tn], [s32_d], [T32])
        for (c0, m, dst) in ((0, 128, CQ[:, 0, 0:tn]), (128, 128, CQ[:, 1, 0:tn]), (256, 128, CKV[:, 0:tn]), (384, 32, KR[:, 0, 0:tn])):
            ps = next_ps()
            for k in range(8):
                P.mm(ps[0:m, 0:tn], wm[:, k, c0:c0 + m], hT[:, k, t0:t0 + tn], k == 0, k == 7, [wm, hT], [ps])
            evac(dst, ps[0:m, 0:tn], [ps], [CQ, CKV, KR])
        ps = next_ps()
        for k in range(8):
            P.mm(ps[0:32, 0:tn], wk[:, k, 0:32], hT[:, k, t0:t0 + tn], k == 0, k == 7, [wk, hT], [ps])
        evac(KR[:, 1, 0:tn], ps[0:32, 0:tn], [ps], [KR])
        P.act(SQ[:, 0:2, 0:tn], CQ[:, :, 0:tn], AF.Square, [CQ], [SQ])
        P.act(SQ[:, 2, 0:tn], CKV[:, 0:tn], AF.Square, [CKV], [SQ])
        P.mm(psB[:, 0:tn], ones[:], SQ[:, 0, 0:tn], True, False, [ones, SQ], [psB])
        P.mm(psB[:, 0:tn], ones[:], SQ[:, 1, 0:tn], False, True, [ones, SQ], [psB])
        P.act(RS[:, 0, 0:tn], psB[:, 0:tn], AF.Sqrt, [psB], [RS], bias=RMS_EPS, scale=1.0 / 256)
        P.mm(psB[:, 0:tn], ones[:], SQ[:, 2, 0:tn], True, True, [ones, SQ, RS], [psB])
        P.act(RS[:, 1, 0:tn], psB[:, 0:tn], AF.Sqrt, [psB], [RS], bias=RMS_EPS, scale=1.0 / 128)
        P.op("dve", lambda RS=RS, tn=tn: nc.vector.reciprocal(RS[:, :, 0:tn], RS[:, :, 0:tn]), [RS], [RS])
        P.tt("dve", CQN[:, :, 0:tn], CQ[:, :, 0:tn], RS[:, 0:1, 0:tn].to_broadcast([128, 2, tn]), ALU.mult, [CQ, RS], [CQN])
        P.tt("pool", CKVN[:, 0:tn], CKV[:, 0:tn], RS[:, 1, 0:tn], ALU.mult, [CKV, RS], [CKVN])
        for h in range(8):
            i = nq["n"] % 2
            nq["n"] += 1
            pa, pb = next_ps(), next_ps()
            for k in range(2):
                P.mm(pa[0:96, 0:tn], uq[:, k, h * 96:(h + 1) * 96], CQN[:, k, 0:tn], k == 0, k == 1, [uq, CQN], [pa])
            for k in range(2):
                P.mm(pb[0:96, 0:tn], uq2[:, k, h * 96:(h + 1) * 96], CQN[:, k, 0:tn], k == 0, k == 1, [uq2, CQN], [pb])
            P.tt("dve", t1[i][:, 0:tn], pa[0:96, 0:tn], T96[:, 0, 0:tn], ALU.mult, [pa, T96], [t1[i]])
            P.tt("dve", t2[i][:, 0:tn], pb[0:96, 0:tn], T96[:, 1, 0:tn], ALU.mult, [pb, T96], [t2[i]])
            P.tt("pool", qo[i][:, 0:tn], t1[i][:, 0:tn], t2[i][:, 0:tn], ALU.add, [t1[i], t2[i]], [qo[i]])
            P.dma("pool", mq_o[h, :, t0:t0 + tn], qo[i][:, 0:tn], [qo[i]], [mq_o])
        for hp in range(4):
            i = nq["n"] % 2
            nq["n"] += 1
            ps = next_ps()
            P.mm(ps[:, 0:tn], ukvk[:].rearrange("p h c -> p (h c)")[:, hp * 128:(hp + 1) * 128], CKVN[:, 0:tn], True, True, [ukvk, CKVN], [ps])
            evac(kno[i][:, 0:tn], ps[:, 0:tn], [ps], [kno[i]])
            P.dma("pool", mkn_o[hp * 128:(hp + 1) * 128, t0:t0 + tn], kno[i][:, 0:tn], [kno[i]], [mkn_o])
        for tt_ in range(tn // 128):
            i = nq["n"] % 2
            nq["n"] += 1
            ps = next_ps()
            P.mm(ps[:, :], CKVN[:, tt_ * 128:(tt_ + 1) * 128], ukvv[:].rearrange("p h c -> p (h c)"), True, True, [ukvv, CKVN], [ps])
            evac(vo[i][:], ps[:], [ps], [vo[i]])
            P.dma("pool", mv_o[t0 + tt_ * 128:t0 + (tt_ + 1) * 128, :], vo[i][:], [vo[i]], [mv_o])
        i = bi % 2
        P.tt("dve", t1[i][0:32, 0:tn], KR[:, 0, 0:tn], T32[:, 0, 0:tn], ALU.mult, [KR, T32], [t1[i]])
        P.tt("dve", t2[i][0:32, 0:tn], KR[:, 1, 0:tn], T32[:, 1, 0:tn], ALU.mult, [KR, T32], [t2[i]])
        P.tt("pool", kpo[i][:, 0:tn], t1[i][0:32, 0:tn], t2[i][0:32, 0:tn], ALU.add, [t1[i], t2[i]], [kpo[i]])
        P.dma("pool", mkpe_o[:, t0:t0 + tn], kpo[i][:, 0:tn], [kpo[i]], [mkpe_o])

    P.fence("pool", outs)
    P.emit()
    return P


import math

NCH = 1056
CB = [(0, 512), (512, 512), (1024, 32)]
PI = math.pi
CW1 = 6.28125
CW2 = 2 * math.pi - 6.28125


def build_s5():
    P = Prog()
    nc = P.nc
    u_d = P.dram_in("u", [4, 2, 2, 128, NCH])
    par_d = P.dram_in("par", [128, 12])
    bc_d = P.dram_in("bc", [128, 4, 4, 16])
    mask_d = P.dram_in("tmask", [128, 128])
    idd = P.dram_in("ident", [128, 128])
    y_o = P.dram_out("y", [4, 2, 2, 128, NCH])

    ident = P.sbuf("ident", [128, 128])
    tmask = P.sbuf("tmask", [128, 128])
    par = P.sbuf("par", [128, 12])
    bcp = P.sbuf("bcp", [128, 4, 4, 16])
    P.dma("sp", ident[:], idd[:], [idd], [ident])
    P.dma("sp", tmask[:], mask_d[:], [mask_d], [tmask])
    P.dma("sp", par[:], par_d[:], [par_d], [par])
    P.dma("sp", bcp[:], bc_d[:], [bc_d], [bcp])

    sm = P.sbuf("sm", [128, 32, 4])
    SM = [sm]

    def col(i):
        return sm[:, i, :]
    (DT, ZR, ANG, MAG, KK, R, R2, WW, SIN, COS, ARE, AIM, NR, DEN, RDEN, FR, FI, T1, T2, T3, M2, RM, I8R, I8I) = range(24)
    lr, li, ldt = par[:, 0:4], par[:, 4:8], par[:, 8:12]
    D = "dve"
    P.act(col(DT), ldt, AF.Exp, [par], SM)
    P.tt(D, col(ZR), lr, col(DT), ALU.mult, [par] + SM, SM)
    P.tt(D, col(ANG), li, col(DT), ALU.mult, [par] + SM, SM)
    P.act(col(MAG), col(ZR), AF.Exp, SM, SM)
    P.ts(D, col(KK), col(ANG), PI, None, ALU.is_gt, None, SM, SM)
    for m in (1, 2, 3):
        P.stt(D, col(KK), col(ANG), (2 * m + 1) * PI, col(KK), ALU.is_gt, ALU.add, SM, SM)
    P.stt(D, col(R), col(KK), -CW1, col(ANG), ALU.mult, ALU.add, SM, SM)
    P.stt(D, col(R), col(KK), -CW2, col(R), ALU.mult, ALU.add, SM, SM)
    P.act(col(SIN), col(R), AF.Sin, SM, SM)
    P.ts(D, col(R2), col(R), PI / 2, None, ALU.add, None, SM, SM)
    P.ts(D, col(WW), col(R2), PI, None, ALU.is_gt, None, SM, SM)
    P.stt(D, col(R2), col(WW), -2 * PI, col(R2), ALU.mult, ALU.add, SM, SM)
    P.act(col(COS), col(R2), AF.Sin, SM, SM)
    P.tt(D, col(ARE), col(MAG), col(COS), ALU.mult, SM, SM)
    P.tt(D, col(AIM), col(MAG), col(SIN), ALU.mult, SM, SM)
    P.ts(D, col(NR), col(ARE), -1.0, None, ALU.add, None, SM, SM)
    P.tt(D, col(T1), lr, lr, ALU.mult, [par] + SM, SM)
    P.tt(D, col(T2), li, li, ALU.mult, [par] + SM, SM)
    P.tt(D, col(DEN), col(T1), col(T2), ALU.add, SM, SM)
    P.op(D, lambda: nc.vector.reciprocal(col(RDEN), col(DEN)), SM, SM)
    P.tt(D, col(T1), col(NR), lr, ALU.mult, [par] + SM, SM)
    P.tt(D, col(T2), col(AIM), li, ALU.mult, [par] + SM, SM)
    P.tt(D, col(T1), col(T1), col(T2), ALU.add, SM, SM)
    P.tt(D, col(FR), col(T1), col(RDEN), ALU.mult, SM, SM)
    P.tt(D, col(T1), col(AIM), lr, ALU.mult, [par] + SM, SM)
    P.tt(D, col(T2), col(NR), li, ALU.mult, [par] + SM, SM)
    P.tt(D, col(T1), col(T1), col(T2), ALU.subtract, SM, SM)
    P.tt(D, col(FI), col(T1), col(RDEN), ALU.mult, SM, SM)

    big = [P.sbuf(f"big{i}", [128, 4, 128]) for i in range(2)]

    def cmul(o_re, o_im, a_re, a_im, b_re, b_im, shape, reads, writes, neg_im=False):
        n = 1
        for s_ in shape[1:]:
            n *= s_
        def tv(t):
            v = t[:].rearrange("p a b -> p (a b)")[:, 0:n]
            if len(shape) == 3:
                return v.rearrange("p (a b) -> p a b", a=shape[1])
            if len(shape) == 4:
                return v.rearrange("p (a b c) -> p a b c", a=shape[1], b=shape[2])
            return v
        t1, t2 = tv(big[0]), tv(big[1])
        P.tt(D, t1, a_re, b_re, ALU.mult, reads, [big[0]])
        P.tt(D, t2, a_im, b_im, ALU.mult, reads, [big[1]])
        P.tt(D, o_re, t1, t2, ALU.subtract, big, writes)
        P.tt(D, t1, a_re, b_im, ALU.mult, reads + writes, [big[0]])
        P.tt(D, t2, a_im, b_re, ALU.mult, reads + writes, [big[1]])
        if neg_im:
            P.stt(D, o_im, t1, -1.0, t2, ALU.mult, ALU.subtract, big, writes)
        else:
            P.tt(D, o_im, t1, t2, ALU.add, big, writes)

    bb = P.sbuf("bb", [128, 2, 4, 16])
    fr_b = col(FR).unsqueeze(2).to_broadcast([128, 4, 16])
    fi_b = col(FI).unsqueeze(2).to_broadcast([128, 4, 16])
    cmul(bb[:, 0], bb[:, 1], fr_b, fi_b, bcp[:, :, 0, :], bcp[:, :, 1, :], [128, 4, 16], SM + [bcp], [bb])

    pw = P.sbuf("pw", [128, 2, 4, 9])
    P.memset("pool", pw[:, 0, :, 0:1], 1.0, [pw])
    P.memset("pool", pw[:, 1, :, 0:1], 0.0, [pw])
    P.copy(D, pw[:, 0, :, 1], col(ARE), SM, [pw])
    P.copy(D, pw[:, 1, :, 1], col(AIM), SM, [pw])
    for (lo, n, by) in ((2, 1, 1), (3, 2, 2), (5, 4, 4)):
        cmul(pw[:, 0, :, lo:lo + n], pw[:, 1, :, lo:lo + n], pw[:, 0, :, 1:1 + n], pw[:, 1, :, 1:1 + n],
             pw[:, 0, :, by:by + 1].to_broadcast([128, 4, n]), pw[:, 1, :, by:by + 1].to_broadcast([128, 4, n]),
             [128, 4, n], [pw], [pw])

    Wc = P.sbuf("Wc", [128, 2, 4, 128])
    R1 = P.sbuf("R1", [128, 2, 4, 128])
    Wq = P.sbuf("Wq", [128, 2, 4, 128])

    def v4(t, c):
        return t[:, c].rearrange("p q (s j) -> p q s j", j=16)
    cmul(v4(Wc, 0), v4(Wc, 1),
         pw[:, 0, :, 0:8].unsqueeze(3).to_broadcast([128, 4, 8, 16]), pw[:, 1, :, 0:8].unsqueeze(3).to_broadcast([128, 4, 8, 16]),
         bb[:, 0].unsqueeze(2).to_broadcast([128, 4, 8, 16]), bb[:, 1].unsqueeze(2).to_broadcast([128, 4, 8, 16]),
         [128, 4, 8, 16], [pw, bb], [Wc])
    cmul(v4(R1, 0), v4(R1, 1),
         pw[:, 0, :, 1:9].unsqueeze(3).to_broadcast([128, 4, 8, 16]), pw[:, 1, :, 1:9].unsqueeze(3).to_broadcast([128, 4, 8, 16]),
         bcp[:, :, 2, :].unsqueeze(2).to_broadcast([128, 4, 8, 16]), bcp[:, :, 3, :].unsqueeze(2).to_broadcast([128, 4, 8, 16]),
         [128, 4, 8, 16], [pw, bcp], [R1], neg_im=True)
    p8r, p8i = pw[:, 0, :, 8], pw[:, 1, :, 8]
    P.tt(D, col(T1), p8r, p8r, ALU.mult, [pw], SM)
    P.tt(D, col(T2), p8i, p8i, ALU.mult, [pw], SM)
    P.tt(D, col(M2), col(T1), col(T2), ALU.add, SM, SM)
    P.op(D, lambda: nc.vector.reciprocal(col(RM), col(M2)), SM, SM)
    P.tt(D, col(I8R), p8r, col(RM), ALU.mult, [pw] + SM, SM)
    P.stt(D, col(I8I), p8i, -1.0, col(RM), ALU.mult, ALU.mult, [pw] + SM, SM)
    cmul(Wq[:, 0], Wq[:, 1], col(I8R).unsqueeze(2).to_broadcast([128, 4, 128]), col(I8I).unsqueeze(2).to_broadcast([128, 4, 128]),
         Wc[:, 0], Wc[:, 1], [128, 4, 128], SM + [Wc], [Wq])

    NST = 11
    AA = P.sbuf("AA", [128, 3, 4, NST])
    P.copy(D, AA[:, 0, :, 0], p8r, [pw], [AA])
    P.copy(D, AA[:, 1, :, 0], p8i, [pw], [AA])
    for j in range(1, NST):
        P.tt(D, col(T1), AA[:, 0, :, j - 1], AA[:, 0, :, j - 1], ALU.mult, [AA], SM)
        P.tt(D, col(T2), AA[:, 1, :, j - 1], AA[:, 1, :, j - 1], ALU.mult, [AA], SM)
        P.tt(D, AA[:, 0, :, j], col(T1), col(T2), ALU.subtract, SM, [AA])
        P.stt(D, AA[:, 1, :, j], AA[:, 0, :, j - 1], 2.0, AA[:, 1, :, j - 1], ALU.mult, ALU.mult, [AA], [AA])
    P.ts(D, AA[:, 2], AA[:, 1], -1.0, None, ALU.mult, None, [AA], [AA])

    WtT = P.sbuf("WtT", [128, 4, 2, 128], BF16)
    Tt = P.sbuf("Tt", [128, 4, 2, 128], BF16)
    R1b = P.sbuf("R1b", [128, 2, 4, 128], BF16)
    ps = [P.psum(f"ps{i}", [128, 512]) for i in range(6)]
    rr = {"n": 0}

    def nps():
        rr["n"] += 1
        return ps[rr["n"] % 6]
    P.copy("pool", R1b[:], R1[:], [R1], [R1b])
    for q in range(4):
        for c in range(2):
            pt = nps()
            P.tr(pt[:, 0:128], Wc[:, c, q, :], ident[:], [Wc, ident], [pt])
            P.copy("act", WtT[:, q, c, :], pt[:, 0:128], [pt], [WtT])
        for d in range(2):
            pt = nps()
            sl = slice(d * 64, d * 64 + 64)
            P.mm(pt[:, 0:128], Wq[sl, 0, q, :], R1[sl, 0, q, :], True, False, [Wq, R1], [pt])
            P.mm(pt[:, 0:128], Wq[sl, 1, q, :], R1[sl, 1, q, :], False, True, [Wq, R1], [pt])
            P.tt(D, Tt[:, q, d, :], pt[:, 0:128], tmask[:], ALU.mult, [pt, tmask], [Tt])

    ust = [P.sbuf(f"ust{i}", [128, NCH]) for i in range(2)]
    ub = [[[P.sbuf(f"ub{i}{d}{b}", [128, NCH], BF16) for b in range(2)] for d in range(2)] for i in range(2)]
    S = [[P.sbuf(f"S{i}{c}", [128, 2, NCH + 1]) for c in range(2)] for i in range(2)]
    Sb = [P.sbuf(f"Sb{c}", [128, 2, NCH + 1], BF16) for c in range(2)]
    yst = [P.sbuf(f"yst{i}", [128, NCH]) for i in range(2)]
    nu = {"n": 0}
    for q in range(4):
        U = ub[q % 2]
        for d in range(2):
            for b in range(2):
                i = nu["n"] % 2
                nu["n"] += 1
                P.dma("sp", ust[i][:], u_d[q, d, b], [u_d], [ust[i]])
                P.copy("pool", U[d][b][:], ust[i][:], [ust[i]], [U[d][b]])
        cur = S[0]
        for c in range(2):
            P.memset("pool", cur[c][:, :, 0:1], 0.0, [cur[c]])
        for b in range(2):
            for (k0, kn) in CB:
                for c in range(2):
                    pt = nps()
                    for d in range(2):
                        P.mm(pt[d * 64:d * 64 + 64, 0:kn], WtT[:, q, c, d * 64:d * 64 + 64], U[d][b][:, k0:k0 + kn], True, True,
                             [WtT, U[d][b]], [pt])
                    P.copy("act", cur[c][:, b, 1 + k0:1 + k0 + kn], pt[:, 0:kn], [pt], [cur[c]])
        L = NCH + 1
        for j in range(NST):
            sh = 1 << j
            nxt = S[(j + 1) % 2]
            ar, ai, nai = AA[:, 0, q, j:j + 1], AA[:, 1, q, j:j + 1], AA[:, 2, q, j:j + 1]
            for c in range(2):
                P.copy("pool", nxt[c][:, :, 0:sh], cur[c][:, :, 0:sh], [cur[c]], [nxt[c]])
            P.stt(D, nxt[0][:, :, sh:L], cur[0][:, :, 0:L - sh], ar, cur[0][:, :, sh:L], ALU.mult, ALU.add, [cur[0], AA], [nxt[0]])
            P.stt(D, nxt[0][:, :, sh:L], cur[1][:, :, 0:L - sh], nai, nxt[0][:, :, sh:L], ALU.mult, ALU.add, [cur[1], AA], [nxt[0]])
            P.stt(D, nxt[1][:, :, sh:L], cur[1][:, :, 0:L - sh], ar, cur[1][:, :, sh:L], ALU.mult, ALU.add, [cur[1], AA], [nxt[1]])
            P.stt(D, nxt[1][:, :, sh:L], cur[0][:, :, 0:L - sh], ai, nxt[1][:, :, sh:L], ALU.mult, ALU.add, [cur[0], AA], [nxt[1]])
            cur = nxt
        for c in range(2):
            P.copy("act" if c else "pool", Sb[c][:], cur[c][:], [cur[c]], [Sb[c]])
        for d in range(2):
            sl = slice(d * 64, d * 64 + 64)
            for b in range(2):
                i = nu["n"] % 2
                nu["n"] += 1
                for (k0, kn) in CB:
                    pt = nps()
                    P.mm(pt[:, 0:kn], Tt[:, q, d, :], U[d][b][:, k0:k0 + kn], True, False, [Tt, U[d][b]], [pt])
                    P.mm(pt[:, 0:kn], R1b[sl, 0, q, :], Sb[0][sl, b, k0:k0 + kn], False, False, [R1b, Sb[0]], [pt])
                    P.mm(pt[:, 0:kn], R1b[sl, 1, q, :], Sb[1][sl, b, k0:k0 + kn], False, True, [R1b, Sb[1]], [pt])
                    P.copy("act", yst[i][:, k0:k0 + kn], pt[:, 0:kn], [pt], [yst[i]])
                P.dma("pool", y_o[q, d, b], yst[i][:], [yst[i]], [y_o])
    P.fence("pool", [y_o])
    P.emit()
    return P


NTOK = 2304
NKEY = 8448
NKT = 66
MLA_SCALE = 96 ** -0.5
NA_SCALE = 0.125
QB = [(0, 512), (512, 512), (1024, 512), (1536, 512)]


def build_attn():
    P = Prog()
    nc = P.nc
    mq_d = P.dram_in("mq", [8, 96, NTOK], BF16)
    mk_d = P.dram_in("mk", [8, 96, NKEY], BF16)
    mv_d = P.dram_in("mv", [8, 128, NKT * 65], BF16)
    nq_d = P.dram_in("nq", [8, 64, NTOK], BF16)
    nk_d = P.dram_in("nk", [8, 64, 2816], BF16)
    nv_d = P.dram_in("nv", [8, 128, 22 * 65], BF16)
    gp_d = P.dram_in("gp", [8, 128, 1408])
    am_d = P.dram_in("am", [8, 32 * 128], BF16)
    bq_d = P.dram_in("bq", [8, 512], BF16)
    ym_o = P.dram_out("ymT", [512, NTOK])
    yn_o = P.dram_out("ynT", [512, NTOK])

    onesf = P.sbuf("onesf", [128, 64])
    am = P.sbuf("am", [8, 32 * 128], BF16)
    bq = P.sbuf("bq", [8, 512], BF16)
    P.memset("pool", onesf[:], 1.0, [onesf])
    P.dma("sp", am[:], am_d[:], [am_d], [am])
    P.dma("sp", bq[:], bq_d[:], [bq_d], [bq])

    mk = [P.sbuf(f"mk{i}", [96, NKEY], BF16) for i in range(2)]
    mv = [P.sbuf(f"mv{i}", [128, NKT * 65], BF16) for i in range(2)]
    mq = [P.sbuf(f"mq{i}", [96, NTOK], BF16) for i in range(2)]
    nk = [P.sbuf(f"nk{i}", [64, 2816], BF16) for i in range(2)]
    nv = [P.sbuf(f"nv{i}", [128, 22 * 65], BF16) for i in range(2)]
    nq = [P.sbuf(f"nq{i}", [64, NTOK], BF16) for i in range(2)]
    E = [P.sbuf(f"E{i}", [128, 1408]) for i in range(2)]
    pT = [P.sbuf(f"pT{i}", [128, 512], BF16) for i in range(4)]
    tmp = [P.sbuf(f"tmp{i}", [128, 512]) for i in range(2)]
    rc = P.sbuf("rc", [128, 512])
    osb = [P.sbuf(f"osb{i}", [64, 512]) for i in range(2)]
    ost = [P.sbuf(f"ost{i}", [64, 512]) for i in range(2)]
    psS = [P.psum(f"psS{i}", [128, 512]) for i in range(4)]
    psO = [P.psum(f"psO{i}", [128, 512]) for i in range(2)]
    psB = [P.psum(f"psB{i}", [128, 512]) for i in range(2)]
    rr = {"s": 0, "p": 0, "o": 0, "t": 0, "f": 0}

    def nxt(lst, key):
        rr[key] += 1
        return lst[rr[key] % len(lst)]

    def finish(O, n, dst_ap, dst_res):
        B_ = nxt(psB, "f")
        i = rr["f"] % 2
        P.op("dve", lambda: nc.vector.reciprocal(rc[64:65, 0:n], O[64:65, 0:n]), [O], [rc])
        P.mm(B_[0:64, 0:n], onesf[64:65, 0:64], rc[64:65, 0:n], True, True, [onesf, rc], [B_])
        P.copy("act", osb[i][:, 0:n], O[0:64, 0:n], [O], [osb[i]])
        P.tt("dve", ost[i][:, 0:n], osb[i][:, 0:n], B_[0:64, 0:n], ALU.mult, [osb[i], B_], [ost[i]])
        P.dma("pool", dst_ap, ost[i][:, 0:n], [ost[i]], [dst_res])

    SKEW = 2
    tiles = []

    def load_head(h):
        b2 = h % 2
        P.dma("sp", mq[b2][:], mq_d[h], [mq_d], [mq[b2]])
        P.dma("sp", mk[b2][:], mk_d[h], [mk_d], [mk[b2]])
        P.dma("sp", mv[b2][:], mv_d[h], [mv_d], [mv[b2]])
        P.dma("sp", nq[b2][:], nq_d[h], [nq_d], [nq[b2]])
        P.dma("sp", nk[b2][:], nk_d[h], [nk_d], [nk[b2]])
        P.dma("sp", nv[b2][:], nv_d[h], [nv_d], [nv[b2]])
        P.dma("sp", E[b2][:], gp_d[h], [gp_d], [E[b2]])

    def exp_E(h):
        P.act(E[h % 2][:], E[h % 2][:], AF.Exp, [E[h % 2]], [E[h % 2]])

    def mla_tile(h, q0, qn_, kt, first, last, O, pre=None):
        b2 = h % 2
        MK, MV, MQ = mk[b2], mv[b2], mq[b2]
        st = {}

        def front():
            if pre is not None:
                pre()
            S = nxt(psS, "s")
            pt = nxt(pT, "p")
            st["pt"] = pt
            P.mm(S[:, 0:qn_], MK[:, kt * 128:(kt + 1) * 128], MQ[:, q0:q0 + qn_], True, True, [MK, MQ], [S])
            P.act(pt[:, 0:qn_], S[:, 0:qn_], AF.Exp, [S], [pt], scale=MLA_SCALE)

        def back():
            pt = st["pt"]
            P.mm(O[0:65, 0:qn_], MV[:, kt * 65:(kt + 1) * 65], pt[:, 0:qn_], first, last, [MV, pt], [O])
        post = (lambda: finish(O, qn_, ym_o[h * 64:(h + 1) * 64, q0:q0 + qn_], ym_o)) if last else None
        return front, back, post

    def na_tile(h, q0, qn_, kcol, vt, first, last, O, bias_i, am_i, pre=None):
        b2 = h % 2
        NK, NV, NQ, EE = nk[b2], nv[b2], nq[b2], E[b2]
        st = {}

        def front():
            if pre is not None:
                pre()
            S = nxt(psS, "s")
            pt = nxt(pT, "p")
            st["pt"] = pt
            if bias_i is None:
                P.mm(S[:, 0:qn_], NK[:, kcol:kcol + 128], NQ[:, q0:q0 + qn_], True, True, [NK, NQ], [S])
                P.act(pt[:, 0:qn_], S[:, 0:qn_], AF.Exp, [S], [pt], scale=NA_SCALE)
            else:
                tp = nxt(tmp, "t")
                P.mm(S[:, 0:512], NK[:, kcol:kcol + 128], NQ[:, q0:q0 + 512], True, False, [NK, NQ], [S])
                P.mm(S[:, 0:512], am[:, am_i * 128:(am_i + 1) * 128], bq[:], False, True, [am, bq], [S])
                P.act(tp[:], S[:], AF.Exp, [S], [tp], scale=NA_SCALE)
                P.tt("dve", pt[:], tp[:], EE[:, (14 - 2 * bias_i) * 64:(14 - 2 * bias_i) * 64 + 512], ALU.mult, [tp, EE], [pt])

        def back():
            pt = st["pt"]
            P.mm(O[0:65, 0:qn_], NV[:, vt * 65:(vt + 1) * 65], pt[:, 0:qn_], first, last, [NV, pt], [O])
        post = (lambda: finish(O, qn_, yn_o[h * 64:(h + 1) * 64, q0:q0 + qn_], yn_o)) if last else None
        return front, back, post

    load_head(0)
    exp_E(0)
    load_head(1)
    for h in range(8):
        cnt_h = 0
        for (q0, qn_, tl) in [(q0, qn_, list(range(NKT))) for (q0, qn_) in QB] + [(2048, 256, [64, 65])]:
            O = nxt(psO, "o")
            for ti, kt in enumerate(tl):
                pre = None
                if cnt_h == 8 and h >= 1 and h + 1 < 8:
                    pre = (lambda hh=h + 1: load_head(hh))
                cnt_h += 1
                tiles.append(mla_tile(h, q0, qn_, kt, ti == 0, ti == len(tl) - 1, O, pre))
        first_na = True
        for blk, (q0, qn_) in enumerate(QB):
            O = nxt(psO, "o")
            for i in range(10):
                pre = None
                if first_na and h + 1 < 8:
                    pre = (lambda hh=h + 1: exp_E(hh))
                first_na = False
                if i < 8:
                    tiles.append(na_tile(h, q0, 512, (4 * blk + i) * 128, 4 * blk + i, i == 0, False, O, i, blk * 8 + i, pre))
                else:
                    c = i - 8
                    tiles.append(na_tile(h, q0, 512, 2560 + c * 128, 20 + c, False, c == 1, O, None, None, pre))
        O = nxt(psO, "o")
        for c in range(2):
            tiles.append(na_tile(h, 2048, 256, 2560 + c * 128, 20 + c, c == 0, c == 1, O, None, None))
    n = len(tiles)
    for step in range(n + SKEW + 2):
        if step < n:
            tiles[step][0]()
        if SKEW <= step < n + SKEW:
            tiles[step - SKEW][1]()
        if step >= SKEW + 2 and tiles[step - SKEW - 2][2] is not None:
            tiles[step - SKEW - 2][2]()

    P.fence("pool", [ym_o, yn_o])
    P.emit()
    return P


NTOK = 2304
TB = [(0, 512), (512, 512), (1024, 512), (1536, 512), (2048, 256)]
ALPHA = 8 ** 0.25
LN_EPS = 1e-5


def build_l3():
    P = Prog()
    nc = P.nc
    x_d = P.dram_in("x", [NTOK, 1024])
    yf_d = P.dram_in("yfT", [512, NTOK])
    yb_d = P.dram_in("ybT", [512, NTOK])
    ua_d = P.dram_in("uaT", [512, NTOK])
    z_d = P.dram_in("zT", [1536, NTOK])
    yn_d = P.dram_in("ynT", [512, NTOK])
    ym_d = P.dram_in("ymT", [512, NTOK])
    g_d = P.dram_in("gT", [3072, NTOK])
    csc_d = P.dram_in("csc", [128, 16])
    dsk_d = P.dram_in("dskip", [128, 4])
    bglu_d = P.dram_in("bglu", [128, 4])
    wglu_d = P.dram_in("w_glu", [512, 512])
    wa_d = P.dram_in("w_a", [512, 1024])
    wb_d = P.dram_in("w_b", [512, 1024])
    wc_d = P.dram_in("w_c", [512, 1024])
    wo_d = P.dram_in("w_o", [1024, 1024])
    wg_d = P.dram_in("w_gate", [1024, 1024])
    bg_d = P.dram_in("b_gate", [128, 1024])
    lng_d = P.dram_in("ln_g", [128, 1024])
    lnb_d = P.dram_in("ln_b", [128, 1024])
    xo = P.dram_out("xo", [NTOK, 1024])

    wst = P.sbuf("wst", [128, 8, 512])
    wglu = P.sbuf("wglu", [128, 4, 512], BF16)
    wbr = [P.sbuf(f"wbr{i}", [128, 4, 1024], BF16) for i in range(3)]
    wout = P.sbuf("wout", [128, 8, 1024], BF16)
    csc = P.sbuf("csc", [128, 16]); sc = P.sbuf("sc", [128, 16])
    dsk = P.sbuf("dsk", [128, 4]); bglu = P.sbuf("bglu", [128, 4])
    gate = P.sbuf("gate", [128, 2, 1024])
    lng = P.sbuf("lng", [128, 1024]); lnb = P.sbuf("lnb", [128, 1024])
    tbr = [P.sbuf(f"tbr{i}", [128, 4, NTOK], BF16) for i in range(3)]
    ps = [P.psum(f"ps{i}", [128, 512]) for i in range(8)]
    rr = {"p": 0, "l": 0}

    def nps():
        rr["p"] += 1
        return ps[rr["p"] % 8]

    for (t, d) in ((csc, csc_d), (dsk, dsk_d), (bglu, bglu_d), (lng, lng_d), (lnb, lnb_d)):
        P.dma("sp", t[:], d[:], [d], [t])
    P.dma("sp", gate[:, 0, :], bg_d[:], [bg_d], [gate])
    P.dma("sp", gate[:, 1, :], bg_d[:], [bg_d], [gate])
    P.act(sc[:], csc[:], AF.Silu, [csc], [sc])
    gg = P.sbuf("gg", [128, 4, 512]); scb = Tile(gg.h[:].rearrange("p a b -> p (a b)").rearrange("p (c k m) -> p c k m", c=2, k=8), "scb"); scb.res = gg.res
    sc3 = sc[:].rearrange("p (k c) -> p k c", c=2)
    for c in range(2):
        P.copy("dve", scb[:, c], sc3[:, :, c:c + 1].to_broadcast([128, 8, 128]), [sc], [scb])

    def load_cast(dst3, src_ap, src_res, nk, width):
        P.dma("sp", wst[:, 0:nk, 0:width], src_ap.rearrange("(k p) n -> p k n", p=128), [src_res], [wst])
        h = nk // 2
        P.copy("pool", dst3[:, 0:h, :], wst[:, 0:h, 0:width], [wst], [dst3.tensor_res])
        P.copy("dve", dst3[:, h:nk, :], wst[:, h:nk, 0:width], [wst], [dst3.tensor_res])

    class V:
        def __init__(self, ap, res):
            self.ap = ap; self.tensor_res = res
        def __getitem__(self, k):
            return self.ap[k]

    for hf in range(2):
        P.dma("sp", wst[:], wg_d[:, hf * 512:(hf + 1) * 512].rearrange("(k p) n -> p k n", p=128), [wg_d], [wst])
        for c in range(2):
            pt = nps()
            for k in range(8):
                P.mm(pt[:], scb[:, c, k, :], wst[:, k, :], k == 0, k == 7, [scb, wst], [pt])
            P.tt("dve", gate[:, c, hf * 512:(hf + 1) * 512], gate[:, c, hf * 512:(hf + 1) * 512], pt[:], ALU.add, [gate, pt], [gate])
    load_cast(V(wglu[:], wglu), wglu_d[:, :], wglu_d, 4, 512)
    for i, wd in enumerate((wa_d, wb_d, wc_d)):
        for hf in range(2):
            load_cast(V(wbr[i][:, :, hf * 512:(hf + 1) * 512], wbr[i]), wd[:, hf * 512:(hf + 1) * 512], wd, 4, 512)
    for hf in range(2):
        load_cast(V(wout[:, :, hf * 512:(hf + 1) * 512], wout), wo_d[:, hf * 512:(hf + 1) * 512], wo_d, 8, 512)

    ld = [P.sbuf(f"ld{i}", [128, 512]) for i in range(6)]

    def load(src_ap, src_res):
        rr["l"] += 1
        t = ld[rr["l"] % 6]
        P.dma("sp", t[:, 0:src_ap.shape[1]], src_ap, [src_res], [t])
        return t
    ggb = P.sbuf("ggb", [128, 4, 512], BF16)
    sg = [P.sbuf(f"sg{i}", [128, 512]) for i in range(2)]
    sz = [P.sbuf(f"sz{i}", [128, 512]) for i in range(2)]
    n2 = {"n": 0}
    for (t0, tn) in TB:
        for ct in range(4):
            rs_ = slice(ct * 128, (ct + 1) * 128)
            a = load(yf_d[rs_, t0:t0 + tn], yf_d); b = load(yb_d[rs_, t0:t0 + tn], yb_d); u = load(ua_d[rs_, t0:t0 + tn], ua_d)
            P.tt("pool", a[:, 0:tn], a[:, 0:tn], b[:, 0:tn], ALU.add, [a, b], [a])
            P.stt("dve", a[:, 0:tn], u[:, 0:tn], dsk[:, ct:ct + 1], a[:, 0:tn], ALU.mult, ALU.add, [u, dsk, a], [a])
            P.act(gg[:, ct, 0:tn], a[:, 0:tn], AF.Gelu, [a], [gg])
            P.copy("pool", ggb[:, ct, 0:tn], gg[:, ct, 0:tn], [gg], [ggb])
        for co in range(4):
            i = n2["n"] % 2
            n2["n"] += 1
            pt = nps()
            for ci in range(4):
                P.mm(pt[:, 0:tn], wglu[:, ci, co * 128:(co + 1) * 128], ggb[:, ci, 0:tn], ci == 0, ci == 3, [wglu, ggb], [pt])
            P.act(sg[i][:, 0:tn], pt[:, 0:tn], AF.Sigmoid, [pt, bglu], [sg[i]], bias=bglu[:, co:co + 1], scale=1.0)
            z = load(z_d[co * 128:(co + 1) * 128, t0:t0 + tn], z_d)
            P.act(sz[i][:, 0:tn], z[:, 0:tn], AF.Silu, [z], [sz[i]])
            P.tt("dve", sg[i][:, 0:tn], sg[i][:, 0:tn], gg[:, co, 0:tn], ALU.mult, [sg[i], gg], [sg[i]])
            P.tt("pool", tbr[0][:, co, t0:t0 + tn], sg[i][:, 0:tn], sz[i][:, 0:tn], ALU.mult, [sg[i], sz[i]], [tbr[0]])
        for bi, (yd, zoff) in enumerate(((yn_d, 512), (ym_d, 1024))):
            for co in range(4):
                i = n2["n"] % 2
                n2["n"] += 1
                y = load(yd[co * 128:(co + 1) * 128, t0:t0 + tn], yd)
                z = load(z_d[zoff + co * 128:zoff + (co + 1) * 128, t0:t0 + tn], z_d)
                P.act(sz[i][:, 0:tn], z[:, 0:tn], AF.Silu, [z], [sz[i]])
                P.tt("dve" if co % 2 else "pool", tbr[1 + bi][:, co, t0:t0 + tn], y[:, 0:tn], sz[i][:, 0:tn], ALU.mult, [y, sz[i]], [tbr[1 + bi]])

    mT = [P.sbuf("mT", [128, 8, 512], BF16)] * 2
    sgt = [P.sbuf(f"sgt{i}", [128, 512]) for i in range(3)]
    acc = [P.sbuf(f"acc{i}", [128, 512]) for i in range(2)]
    def alias(i, name):
        t = Tile(wst.h[:, 2 * i:2 * i + 2, :].rearrange("p a b -> p (a b)"), name)
        t.res.last_w = wst.res.last_w
        t.res.readers = dict(wst.res.readers)
        return t
    xt = [alias(0, "xt0"), alias(1, "xt1")]
    rt = [alias(2, "rt0"), alias(3, "rt1")]
    st = [P.sbuf(f"st{i}", [128, 2, 6]) for i in range(2)]
    mv = [P.sbuf(f"mv{i}", [128, 4]) for i in range(2)]
    n3 = {"n": 0}
    for bi, (t0, tn) in enumerate(TB):
        MT = mT[bi % 2]
        for f in range(8):
            pts = []
            for br in range(3):
                pt = nps()
                for ci in range(4):
                    P.mm(pt[:, 0:tn], wbr[br][:, ci, f * 128:(f + 1) * 128], tbr[br][:, ci, t0:t0 + tn], ci == 0, ci == 3,
                         [wbr[br], tbr[br]], [pt])
                pts.append(pt)
            A = acc[f % 2]
            for br in range(3):
                g = load(g_d[br * 1024 + f * 128:br * 1024 + (f + 1) * 128, t0:t0 + tn], g_d)
                P.act(sgt[br][:, 0:tn], g[:, 0:tn], AF.Sigmoid, [g], [sgt[br]])
                P.tt("dve", sgt[br][:, 0:tn], sgt[br][:, 0:tn], pts[br][:, 0:tn], ALU.mult, [sgt[br], pts[br]], [sgt[br]])
            P.tt("pool", A[:, 0:tn], sgt[0][:, 0:tn], sgt[1][:, 0:tn], ALU.add, [sgt[0], sgt[1]], [A])
            P.tt("pool", MT[:, f, 0:tn], A[:, 0:tn], sgt[2][:, 0:tn], ALU.add, [A, sgt[2]], [MT])
        for tt_ in range(tn // 128):
            i = n3["n"] % 2
            n3["n"] += 1
            tok0 = t0 + tt_ * 128
            c = 0 if tok0 < 2048 else 1
            X, R, ST, MV = xt[i], rt[i], st[i], mv[i]
            P.dma("sp", X[:], x_d[tok0:tok0 + 128, :], [x_d], [X])
            for hf in range(2):
                pt = nps()
                for k in range(8):
                    P.mm(pt[:], MT[:, k, tt_ * 128:(tt_ + 1) * 128], wout[:, k, hf * 512:(hf + 1) * 512], k == 0, k == 7, [MT, wout], [pt])
                hs = slice(hf * 512, (hf + 1) * 512)
                P.tt("dve", R[:, hs], gate[:, c, hs], pt[:], ALU.mult, [gate, pt], [R])
                P.stt("dve", R[:, hs], X[:, hs], ALPHA, R[:, hs], ALU.mult, ALU.add, [X, R], [R])
            P.op("dve", lambda ST=ST, R=R: nc.vector.bn_stats(ST[:, 0, :], R[:, 0:512]), [R], [ST])
            P.op("dve", lambda ST=ST, R=R: nc.vector.bn_stats(ST[:, 1, :], R[:, 512:1024]), [R], [ST])
            P.op("dve", lambda ST=ST, MV=MV: nc.vector.bn_aggr(MV[:, 0:2], ST[:]), [ST], [MV])
            P.act(MV[:, 2:3], MV[:, 1:2], AF.Sqrt, [MV], [MV], bias=LN_EPS, scale=1.0)
            P.op("dve", lambda MV=MV: nc.vector.reciprocal(MV[:, 3:4], MV[:, 2:3]), [MV], [MV])
            P.ts("dve", R[:], R[:], MV[:, 0:1], MV[:, 3:4], ALU.subtract, ALU.mult, [R, MV], [R])
            P.tt("pool", R[:], R[:], lng[:], ALU.mult, [R, lng], [R])
            P.tt("pool", X[:], R[:], lnb[:], ALU.add, [R, lnb], [X])
            P.dma("pool", xo[tok0:tok0 + 128, :], X[:], [X], [xo])
    P.fence("pool", [xo])
    P.emit()
    return P


_PROGS = {}


def _prog(name, builder):
    if name not in _PROGS:
        _PROGS[name] = builder()
    return _PROGS[name]


def _launch(P, maps):
    res = run_bass_kernel_spmd(P.nc, maps, core_ids=list(range(8)))
    return [{k: np.asarray(v) for k, v in r.items()} for r in res.results]


def kernel(**inputs):
    inp = {k: np.asarray(v) for k, v in inputs.items()}
    x_cur = np.array(inp['x'], dtype=np.float32, copy=True)
    ctx_cur = np.array(inp['ctx'], dtype=np.float32, copy=True)
    for l in range(4):
        r1 = _launch(_prog("l1", build_l1), l1_inputs(inp, l, x_cur, ctx_cur))
        ua_lat, ua_ctx = ua_from_l1(r1)
        r2 = _launch(_prog("s5", build_s5), s5_inputs(inp, l, ua_lat, ua_ctx))
        yf, yb = s5_gather(r2)
        r3 = _launch(_prog("attn", build_attn), attn_inputs(inp, l, r1))
        r4 = _launch(_prog("l3", build_l3), l3_inputs(inp, l, x_cur, ctx_cur, r1, yf, yb, r3))
        for c in range(8):
            b, j = c // 4, c % 4
            x_cur[b, j * 2048:(j + 1) * 2048] = r4[c]['xo'][:2048]
            if j == 0:
                ctx_cur[b] = r4[c]['xo'][2048:]
    return x_cur
```
